# Optimizing a Trainium2 kernel written in Bass

```python
import math
import jax, jax.numpy as jnp
from jax import lax
import numpy as np

D_MODEL = 1024
BATCH = 8
SEQ = 2048
DEPTH = 2
DEC_BATCH = 128
DEC_SEQ = 1
PAST_LEN = 16384
PAGE_SIZE = 128

MIX_WIDTH = D_MODEL
S5_WIDTH = MIX_WIDTH // 2
S5_GROUP = 16
S5_GROUPS = S5_WIDTH // S5_GROUP
S5_STATE = 64
CONV_CH = MIX_WIDTH - S5_WIDTH
CONV_HEADS = 8
CONV_K = 31
MIX_IN = S5_WIDTH + 2 * CONV_CH
D_FF = 2816
FFN_K = 3
EPS = 1e-6
DT_MIN = 1e-3
DT_MAX = 1e-1

kernel_name = "hybrid_s5_conformer_convffn_step"


def rmsnorm(x, g):
    xf = x.astype(jnp.float32)
    y = xf * lax.rsqrt(jnp.mean(xf * xf, axis=-1, keepdims=True) + EPS)
    return (y * g.astype(jnp.float32)).astype(x.dtype)


def layernorm(x, g, b):
    xf = x.astype(jnp.float32)
    mu = jnp.mean(xf, axis=-1, keepdims=True)
    xc = xf - mu
    y = xc * lax.rsqrt(jnp.mean(xc * xc, axis=-1, keepdims=True) + EPS)
    return (y * g.astype(jnp.float32) + b.astype(jnp.float32)).astype(x.dtype)


def causal_dwconv(x, buf, w, b):
    k = w.shape[0]
    xp = jnp.concatenate([buf.astype(x.dtype), x], axis=1)
    y = lax.conv_general_dilated(
        xp, w[:, None, :].astype(x.dtype), window_strides=(1,), padding='VALID',
        dimension_numbers=('NWC', 'WIO', 'NWC'), feature_group_count=x.shape[-1])
    return y + b.astype(x.dtype), xp[:, xp.shape[1] - (k - 1):]


def s5_discretise(lam_re, lam_im, log_dt, b_re, b_im):
    f32 = jnp.float32
    dt = jnp.exp(log_dt.astype(f32))[:, None]
    lr, li = lam_re.astype(f32), lam_im.astype(f32)
    mag = jnp.exp(lr * dt)
    ar, ai = mag * jnp.cos(li * dt), mag * jnp.sin(li * dt)
    den = lr * lr + li * li
    cr = ((ar - 1.0) * lr + ai * li) / den
    ci = (ai * lr - (ar - 1.0) * li) / den
    br, bi = b_re.astype(f32), b_im.astype(f32)
    bbar_r = cr[..., None] * br - ci[..., None] * bi
    bbar_i = cr[..., None] * bi + ci[..., None] * br
    return ar, ai, bbar_r, bbar_i


def _cplx_combine(e1, e2):
    a1r, a1i, b1r, b1i = e1
    a2r, a2i, b2r, b2i = e2
    return (a2r * a1r - a2i * a1i, a2r * a1i + a2i * a1r,
            a2r * b1r - a2i * b1i + b2r, a2r * b1i + a2i * b1r + b2i)


def s5_mixer(u, h0_re, h0_im, lam_re, lam_im, log_dt, b_re, b_im, c_re, c_im, d_skip, w_glu, b_glu):
    bsz, L = u.shape[0], u.shape[1]
    uf = u.astype(jnp.float32)
    ug = uf.reshape(bsz, L, S5_GROUPS, S5_GROUP)
    ar, ai, bbr, bbi = s5_discretise(lam_re, lam_im, log_dt, b_re, b_im)
    xr = jnp.einsum('blgh,gph->blgp', ug, bbr)
    xi = jnp.einsum('blgh,gph->blgp', ug, bbi)
    h0r, h0i = h0_re.astype(jnp.float32), h0_im.astype(jnp.float32)
    xr = xr.at[:, 0].add(ar * h0r - ai * h0i)
    xi = xi.at[:, 0].add(ar * h0i + ai * h0r)
    a_r = jnp.broadcast_to(ar, xr.shape)
    a_i = jnp.broadcast_to(ai, xi.shape)
    _, _, hr, hi = lax.associative_scan(_cplx_combine, (a_r, a_i, xr, xi), axis=1)
    y = (jnp.einsum('blgp,ghp->blgh', hr, c_re.astype(jnp.float32))
         - jnp.einsum('blgp,ghp->blgh', hi, c_im.astype(jnp.float32)))
    y = y.reshape(bsz, L, S5_WIDTH) + d_skip.astype(jnp.float32) * uf
    a = jax.nn.gelu(y, approximate=False)
    out = a * jax.nn.sigmoid(a @ w_glu.astype(jnp.float32) + b_glu.astype(jnp.float32))
    return out.astype(u.dtype), hr[:, -1], hi[:, -1]


def conformer_conv(v, g, buf, conv_w, conv_b, ln_g, ln_b):
    z = v * jax.nn.sigmoid(g)
    c, new_buf = causal_dwconv(z, buf, conv_w, conv_b)
    return jax.nn.silu(layernorm(c, ln_g, ln_b)), new_buf


def run_trunk(x, ssm_re, ssm_im, conv_buf, ffn_buf, W):
    new_re, new_im, new_conv, new_ffn = [], [], [], []
    for l in range(DEPTH):
        h = rmsnorm(x, W['g_pre_mix'][l])
        p = h @ W['w_in'][l]
        u = p[..., :S5_WIDTH]
        cv = p[..., S5_WIDTH:S5_WIDTH + CONV_CH]
        cg = p[..., S5_WIDTH + CONV_CH:]
        s, hr, hi = s5_mixer(u, ssm_re[l], ssm_im[l], W['lam_re'][l], W['lam_im'][l], W['log_dt'][l],
                             W['b_re'][l], W['b_im'][l], W['c_re'][l], W['c_im'][l],
                             W['d_skip'][l], W['w_glu'][l], W['b_glu'][l])
        c, cbuf = conformer_conv(cv, cg, conv_buf[l], W['conv_w'][l], W['conv_b'][l],
                                 W['ln_g'][l], W['ln_b'][l])
        mix = jnp.concatenate([s, c], axis=-1) @ W['w_out'][l]
        x = x + rmsnorm(mix, W['g_post_mix'][l])
        h = rmsnorm(x, W['g_pre_ffn'][l])
        up = h @ W['w_up'][l]
        up, fbuf = causal_dwconv(up, ffn_buf[l], W['ffn_conv_w'][l], W['ffn_conv_b'][l])
        f = (jax.nn.silu(up[..., :D_FF]) * up[..., D_FF:]) @ W['w_down'][l]
        x = x + rmsnorm(f, W['g_post_ffn'][l])
        new_re.append(hr); new_im.append(hi); new_conv.append(cbuf); new_ffn.append(fbuf)
    return x, jnp.stack(new_re), jnp.stack(new_im), jnp.stack(new_conv), jnp.stack(new_ffn)


def setup_inputs(seed: int = 0) -> dict:
    key = jax.random.key(seed)
    ks = iter(jax.random.split(key, 40))
    nrm = lambda shape, s: jax.random.normal(next(ks), shape, jnp.float32) * s
    gain = lambda shape: 1.0 + nrm(shape, 0.02)
    n_idx = jnp.arange(S5_STATE, dtype=jnp.float32)
    inp = {}
    inp['x_prompt'] = nrm((BATCH, SEQ, D_MODEL), 1.0)
    inp['x_sample'] = nrm((DEC_BATCH, DEC_SEQ, D_MODEL), 1.0)
    inp['state_ssm_re'] = nrm((DEPTH, DEC_BATCH, S5_GROUPS, S5_STATE), 0.1)
    inp['state_ssm_im'] = nrm((DEPTH, DEC_BATCH, S5_GROUPS, S5_STATE), 0.1)
    inp['state_conv'] = nrm((DEPTH, DEC_BATCH, CONV_K - 1, CONV_CH), 0.5)
    inp['state_ffn_conv'] = nrm((DEPTH, DEC_BATCH, FFN_K - 1, 2 * D_FF), 1.0)
    inp['g_pre_mix'] = gain((DEPTH, D_MODEL))
    inp['w_in'] = nrm((DEPTH, D_MODEL, MIX_IN), D_MODEL ** -0.5)
    inp['lam_re'] = -0.5 + nrm((DEPTH, S5_GROUPS, S5_STATE), 0.01)
    inp['lam_im'] = math.pi * n_idx + nrm((DEPTH, S5_GROUPS, S5_STATE), 0.01)
    inp['log_dt'] = jax.random.uniform(next(ks), (DEPTH, S5_GROUPS), jnp.float32,
                                       math.log(DT_MIN), math.log(DT_MAX))
    bs = (2.0 * S5_GROUP) ** -0.5
    inp['b_re'] = nrm((DEPTH, S5_GROUPS, S5_STATE, S5_GROUP), bs)
    inp['b_im'] = nrm((DEPTH, S5_GROUPS, S5_STATE, S5_GROUP), bs)
    cs = (2.0 * S5_STATE) ** -0.5
    inp['c_re'] = nrm((DEPTH, S5_GROUPS, S5_GROUP, S5_STATE), cs)
    inp['c_im'] = nrm((DEPTH, S5_GROUPS, S5_GROUP, S5_STATE), cs)
    inp['d_skip'] = nrm((DEPTH, S5_WIDTH), 1.0)
    inp['w_glu'] = nrm((DEPTH, S5_WIDTH, S5_WIDTH), S5_WIDTH ** -0.5)
    inp['b_glu'] = nrm((DEPTH, S5_WIDTH), 0.02)
    inp['conv_w'] = nrm((DEPTH, CONV_K, CONV_CH), CONV_K ** -0.5)
    inp['conv_b'] = nrm((DEPTH, CONV_CH), 0.02)
    inp['ln_g'] = gain((DEPTH, CONV_CH))
    inp['ln_b'] = nrm((DEPTH, CONV_CH), 0.02)
    inp['w_out'] = nrm((DEPTH, MIX_WIDTH, D_MODEL), MIX_WIDTH ** -0.5)
    inp['g_post_mix'] = gain((DEPTH, D_MODEL))
    inp['g_pre_ffn'] = gain((DEPTH, D_MODEL))
    inp['w_up'] = nrm((DEPTH, D_MODEL, 2 * D_FF), D_MODEL ** -0.5)
    inp['ffn_conv_w'] = nrm((DEPTH, FFN_K, 2 * D_FF), FFN_K ** -0.5)
    inp['ffn_conv_b'] = nrm((DEPTH, 2 * D_FF), 0.02)
    inp['w_down'] = nrm((DEPTH, D_FF, D_MODEL), D_FF ** -0.5)
    inp['g_post_ffn'] = gain((DEPTH, D_MODEL))
    return inp


def reference(x_prompt, x_sample, state_ssm_re, state_ssm_im, state_conv, state_ffn_conv,
              g_pre_mix, w_in, lam_re, lam_im, log_dt, b_re, b_im, c_re, c_im, d_skip,
              w_glu, b_glu, conv_w, conv_b, ln_g, ln_b, w_out, g_post_mix, g_pre_ffn,
              w_up, ffn_conv_w, ffn_conv_b, w_down, g_post_ffn):
    W = dict(g_pre_mix=g_pre_mix, w_in=w_in, lam_re=lam_re, lam_im=lam_im, log_dt=log_dt,
             b_re=b_re, b_im=b_im, c_re=c_re, c_im=c_im, d_skip=d_skip, w_glu=w_glu,
             b_glu=b_glu, conv_w=conv_w, conv_b=conv_b, ln_g=ln_g, ln_b=ln_b, w_out=w_out,
             g_post_mix=g_post_mix, g_pre_ffn=g_pre_ffn, w_up=w_up, ffn_conv_w=ffn_conv_w,
             ffn_conv_b=ffn_conv_b, w_down=w_down, g_post_ffn=g_post_ffn)
    dt = x_prompt.dtype
    p_re0 = jnp.zeros((DEPTH, BATCH, S5_GROUPS, S5_STATE), jnp.float32)
    p_conv0 = jnp.zeros((DEPTH, BATCH, CONV_K - 1, CONV_CH), dt)
    p_ffn0 = jnp.zeros((DEPTH, BATCH, FFN_K - 1, 2 * D_FF), dt)
    y_prompt, p_re, p_im, p_conv, p_ffn = run_trunk(x_prompt, p_re0, p_re0, p_conv0, p_ffn0, W)
    y_sample, s_re, s_im, s_conv, s_ffn = run_trunk(x_sample, state_ssm_re, state_ssm_im,
                                                    state_conv, state_ffn_conv, W)
    return (y_prompt, y_sample, p_re, p_im, p_conv, p_ffn, s_re, s_im, s_conv, s_ffn)
```

```python
import math
from contextlib import ExitStack

import numpy as np
import concourse.bass as bass
import concourse.mybir as mybir
from concourse.bass_utils import run_bass_kernel_spmd

F32 = mybir.dt.float32
BF16 = mybir.dt.bfloat16
I32 = mybir.dt.int32
AF = mybir.ActivationFunctionType
ALU = mybir.AluOpType
AX = mybir.AxisListType

D = 1024
KT = 8
SEQ = 2048
NS = 16
DEPTH = 2
DFF = 2816
NUP = 44
EPS = 1e-6
TWO_PI = 2.0 * math.pi


class Res:
    __slots__ = ("name", "w", "r")

    def __init__(self, name="r"):
        self.name = name
        self.w = []
        self.r = []


class Stream:
    def __init__(self, name, engname, sems, step, is_dma):
        self.name = name
        self.engname = engname
        self.sems = sems
        self.step = step
        self.count = 0
        self.is_dma = is_dma

    def semval(self, c):
        if self.is_dma:
            k = len(self.sems)
            return self.sems[(c - 1) % k], 16 * ((c - 1) // k + 1)
        return self.sems[0], c


class Sched:
    def __init__(self):
        self.ops = {"tensor": [], "vector": [], "scalar": [], "gpsimd": [], "sync": []}
        self.streams = {}
        self.waited = {}
        self.waited_dma = set()

    def add_stream(self, name, engname, sems, step, is_dma):
        self.streams[name] = Stream(name, engname, sems, step, is_dma)

    def op(self, stream, fn, reads=(), writes=(), nosync_self=False, signal=True):
        st = self.streams[stream]
        eng = st.engname
        needs = {}
        needs_dma = set()

        def need(w):
            s, c = w
            if nosync_self and s == stream:
                return
            if self.streams[s].is_dma:
                needs_dma.add((s, c))
            elif needs.get(s, 0) < c:
                needs[s] = c

        for r in reads:
            for w in r.w:
                need(w)
        for r in writes:
            for w in r.w:
                if st.is_dma and w[0] == stream:
                    continue
                need(w)
            for w in r.r:
                need(w)
        if st.is_dma and signal and stream == "dq" and st.count >= len(st.sems):
            needs_dma.add((stream, st.count + 1 - len(st.sems)))
        waits = []
        for s, c in needs.items():
            key = (eng, s)
            if self.waited.get(key, 0) >= c:
                continue
            self.waited[key] = c
            waits.append(self.streams[s].semval(c))
        for (s, c) in sorted(needs_dma):
            if self.waited.get((eng, s), 0) >= c or (eng, s, c) in self.waited_dma:
                continue
            self.waited_dma.add((eng, s, c))
            waits.append(self.streams[s].semval(c))
        if signal:
            st.count += 1
            cnew = st.count
        else:
            cnew = st.count + 1
        for r in writes:
            if st.is_dma:
                r.w = [w for w in r.w if w[0] == stream] + [(stream, cnew)]
            else:
                r.w = [(stream, cnew)]
            r.r = []
        for r in reads:
            if st.is_dma:
                r.r.append((stream, cnew))
            else:
                r.r = [w for w in r.r if w[0] != stream] + [(stream, cnew)]
        if signal:
            sem, val = st.semval(cnew)
            self.ops[eng].append((waits, fn, sem, st.step))
        else:
            self.ops[eng].append((waits, fn, None, 0))

    def barrier(self):
        for eng in self.ops:
            waits = []
            for s, st in self.streams.items():
                if st.count == 0 or self.waited.get((eng, s), 0) >= st.count:
                    continue
                self.waited[(eng, s)] = st.count
                if st.is_dma:
                    k = len(st.sems)
                    for j in range(min(k, st.count)):
                        n_j = (st.count - 1 - j) // k + 1
                        waits.append((st.sems[j], 16 * n_j))
                else:
                    waits.append((st.sems[0], st.count))
            if waits:
                self.ops[eng].append((waits, None, None, 0))

    def emit(self, block):
        def mk(engname):
            def body(e):
                for waits, fn, sem, step in self.ops[engname]:
                    for (ws, wv) in waits:
                        e.wait_ge(ws, wv)
                    if fn is not None:
                        ins = fn(e)
                        if sem is not None:
                            ins.then_inc(sem, step)
            return body
        block.tensor(mk("tensor"))
        block.vector(mk("vector"))
        block.scalar(mk("scalar"))
        block.gpsimd(mk("gpsimd"))
        block.sync(mk("sync"))


class Arena:
    def __init__(self, t32, tbf, ti32, nbytes):
        self.t32, self.tbf, self.ti32 = t32, tbf, ti32
        self.nbytes = nbytes
        self.off = 0
        self.peak = 0

    def mark(self):
        return self.off

    def reset(self, m):
        self.off = m

    def alloc(self, dtype, shape):
        n = 1
        for s in shape[1:]:
            n *= s
        size = 2 if dtype == BF16 else 4
        off = self.off
        self.off += ((n * size + 63) // 64) * 64
        self.peak = max(self.peak, self.off)
        assert self.off <= self.nbytes, ("SBUF arena overflow", self.off, self.nbytes)
        self.last_off = off
        return self._view(off, dtype, shape)

    def alias(self, off, dtype, shape):
        return self._view(off, dtype, shape)

    def _view(self, off, dtype, shape):
        n = 1
        for s in shape[1:]:
            n *= s
        size = 2 if dtype == BF16 else 4
        base = {F32: self.t32, BF16: self.tbf, I32: self.ti32}[dtype]
        e0 = off // size
        ap = base[0:shape[0], e0:e0 + n]
        if len(shape) == 3:
            ap = ap.rearrange("p (a b) -> p a b", a=shape[1], b=shape[2])
        elif len(shape) == 4:
            ap = ap.rearrange("p (a b c) -> p a b c", a=shape[1], b=shape[2], c=shape[3])
        elif len(shape) == 5:
            ap = ap.rearrange("p (a b c d) -> p a b c d", a=shape[1], b=shape[2], c=shape[3], d=shape[4])
        return ap


def bcast_ap(ap, axis, n):
    dims = [list(x) for x in ap.ap]
    dims.insert(axis, [0, n])
    return bass.AP(ap.tensor, ap.offset, dims)


def swap2(ap, axis=1):
    dims = [list(x) for x in ap.ap]
    st = dims[axis][0]
    assert dims[axis][1] == 2
    dims[axis] = [-st, 2]
    return bass.AP(ap.tensor, ap.offset + st, dims)


def build_program():
    nc = bass.Bass("TRN2", target_bir_lowering=False)
    dt_in = {}

    def din(name, shape, dtype=F32):
        dt_in[name] = nc.dram_tensor(name, list(shape), dtype, kind="ExternalInput").ap()
        return dt_in[name]

    def dout(name, shape):
        return nc.dram_tensor(name, list(shape), F32, kind="ExternalOutput").ap()

    xp = din("xp", [SEQ, D]); xs = din("xs", [NS, D])
    sre = din("sre", [DEPTH, NS, 2048]); sim = din("sim", [DEPTH, NS, 2048])
    scv = din("scv", [DEPTH, NS, 30, 512]); sff = din("sff", [DEPTH, NS, 2, 2 * DFF])
    g_pre_mix = din("g_pre_mix", [DEPTH, D]); w_in = din("w_in", [DEPTH, D, 1536])
    lam_re = din("lam_re", [DEPTH, 32, 64]); lam_im = din("lam_im", [DEPTH, 32, 64])
    log_dt = din("log_dt", [DEPTH, 32])
    b_re = din("b_re", [DEPTH, 32, 64, 16]); b_im = din("b_im", [DEPTH, 32, 64, 16])
    c_re = din("c_re", [DEPTH, 512, 64]); c_im = din("c_im", [DEPTH, 512, 64])
    d_skip = din("d_skip", [DEPTH, 512]); w_glu = din("w_glu", [DEPTH, 512, 512])
    b_glu = din("b_glu", [DEPTH, 512]); conv_w = din("conv_w", [DEPTH, 31, 512])
    conv_b = din("conv_b", [DEPTH, 512]); ln_g = din("ln_g", [DEPTH, 512]); ln_b = din("ln_b", [DEPTH, 512])
    w_out = din("w_out", [DEPTH, D, D]); g_post_mix = din("g_post_mix", [DEPTH, D])
    g_pre_ffn = din("g_pre_ffn", [DEPTH, D]); w_up = din("w_up", [DEPTH, D, 2 * DFF])
    ffn_conv_w = din("ffn_conv_w", [DEPTH, 3, 2 * DFF]); ffn_conv_b = din("ffn_conv_b", [DEPTH, 2 * DFF])
    w_down = din("w_down", [DEPTH, DFF, D]); g_post_ffn = din("g_post_ffn", [DEPTH, D])
    c_ident = din("c_ident", [128, 128]); c_mask = din("c_mask", [128, 128])

    yp = dout("yp", [SEQ, D]); ys = dout("ys", [NS, D])
    pre = dout("pre", [DEPTH, 32, 64]); pim = dout("pim", [DEPTH, 32, 64])
    pcv = dout("pcv", [DEPTH, 30, 512]); pff = dout("pff", [DEPTH, 2, 2 * DFF])
    sre_o = dout("sre_o", [DEPTH, NS, 2048]); sim_o = dout("sim_o", [DEPTH, NS, 2048])
    scv_o = dout("scv_o", [DEPTH, NS, 30, 512]); sff_o = dout("sff_o", [DEPTH, NS, 2, 2 * DFF])

    ARENA_BYTES = 206 * 1024
    with ExitStack() as es:
        E = es.enter_context
        arena_t = E(nc.sbuf_tensor("arena", [128, ARENA_BYTES // 4], F32))
        AR = Arena(arena_t, arena_t.bitcast(BF16), arena_t.bitcast(I32), ARENA_BYTES)
        banks = [E(nc.psum_tensor("bank%d" % i, [128, 512], F32)) for i in range(4)]
        b45 = E(nc.psum_tensor("bank45", [128, 1024], F32))
        banks = banks + [b45[:, 0:512], b45[:, 512:1024]]
        b67 = E(nc.psum_tensor("bank67", [128, 1024], F32))
        bankR = [Res("bank%d" % i) for i in range(6)]
        b67R = Res("b67")
        S = Sched()
        for name, eng in [("pe", "tensor"), ("dve", "vector"), ("act", "scalar"), ("pool", "gpsimd")]:
            S.add_stream(name, eng, [E(nc.semaphore("s_" + name))], 1, False)
        S.add_stream("dq", "sync", [E(nc.semaphore("s_dq%d" % i)) for i in range(64)], 16, True)
        S.add_stream("dw", "gpsimd", [E(nc.semaphore("s_dw%d" % i)) for i in range(24)], 16, True)

        def OP(stream, meth, reads, writes, *a, **kw):
            nosync = kw.pop("nosync", False) or (stream == "pe")
            sig = kw.pop("signal", True)
            S.op(stream, lambda e: getattr(e, meth)(*a, **kw), reads, writes, nosync_self=nosync, signal=sig)

        def V(meth, reads, writes, *a, **kw):
            OP("dve", meth, reads, writes, *a, **kw)

        def A(meth, reads, writes, *a, **kw):
            OP("act", meth, reads, writes, *a, **kw)

        def G(meth, reads, writes, *a, **kw):
            OP("pool", meth, reads, writes, *a, **kw)

        def PE(meth, reads, writes, *a, **kw):
            OP("pe", meth, reads, writes, *a, **kw)

        def DQ(reads, writes, out, in_, **kw):
            OP("dq", "dma_start", reads, writes, out=out, in_=in_, **kw)

        def DW(reads, writes, out, in_, **kw):
            OP("dw", "dma_start", reads, writes, out=out, in_=in_, **kw)

        def bfv(bank):
            return bank.bitcast(BF16)

        identf = AR.alloc(F32, [128, 128]); identb = AR.alloc(BF16, [128, 128])
        maskf = AR.alloc(F32, [128, 128]); onesf = AR.alloc(F32, [128, 128])
        P512 = AR.alloc(F32, [128, DEPTH, 4, 36])
        P5632 = AR.alloc(F32, [128, DEPTH, NUP, 4])
        P1024 = AR.alloc(F32, [128, DEPTH, KT, 2])
        SC = AR.alloc(F32, [128, 2, 16])
        LRs = AR.alloc(F32, [128, DEPTH, 16]); LIs = AR.alloc(F32, [128, DEPTH, 16]); LDTs = AR.alloc(F32, [128, DEPTH, 16])
        DSCs = AR.alloc(F32, [128, DEPTH, 32]); smallR = Res("small")
        cR = Res("consts"); pR = Res("params"); scR = Res("SC")
        DQ([], [cR], identf, c_ident)
        DQ([], [cR], maskf, c_mask)
        V("tensor_copy", [cR], [cR], out=identb, in_=identf)
        V("memset", [], [cR], onesf, 1.0)

        for l in range(DEPTH):
            DQ([], [smallR], LRs[:, l, :], lam_re[l].rearrange("(gp a) p -> (a p) gp", a=2), allow_slow_non_contiguous=True)
            DQ([], [smallR], LIs[:, l, :], lam_im[l].rearrange("(gp a) p -> (a p) gp", a=2), allow_slow_non_contiguous=True)
            for a in range(2):
                DQ([], [smallR], LDTs[a * 64:(a + 1) * 64, l, :], bass.AP(log_dt.tensor, log_dt[l].offset + a, [[0, 64], [2, 16]]),
                   allow_slow_non_contiguous=True)
            for j in range(8):
                DQ([], [smallR], DSCs[j * 16:(j + 1) * 16, l, :], d_skip[l].rearrange("(g h) -> h g", h=16), allow_slow_non_contiguous=True)
        m0 = AR.mark()
        stg = AR.alloc(F32, [36, 512]); stgR = Res("stg")
        stg2 = AR.alloc(F32, [4, 2 * DFF]); stg2R = Res("stg2")
        stg3 = AR.alloc(F32, [2, D]); stg3R = Res("stg3")
        for l in range(DEPTH):
            DQ([], [stgR], stg[0:31, :], conv_w[l])
            DQ([], [stgR], stg[31:32, :], conv_b[l:l + 1, :])
            DQ([], [stgR], stg[32:33, :], ln_g[l:l + 1, :])
            DQ([], [stgR], stg[33:34, :], ln_b[l:l + 1, :])
            DQ([], [stgR], stg[34:35, :], b_glu[l:l + 1, :])
            DQ([], [stgR], stg[35:36, :], d_skip[l:l + 1, :])
            for c in range(4):
                PE("transpose", [stgR, cR], [bankR[0]], banks[0][:, c * 36:(c + 1) * 36], stg[:, c * 128:(c + 1) * 128], identf[0:36, 0:36])
            V("tensor_copy", [bankR[0]], [pR], out=P512[:, l], in_=banks[0][:, 0:144].rearrange("p (c r) -> p c r", c=4))
            DQ([], [stg2R], stg2[0:3, :], ffn_conv_w[l])
            DQ([], [stg2R], stg2[3:4, :], ffn_conv_b[l:l + 1, :])
            for c in range(NUP):
                PE("transpose", [stg2R, cR], [bankR[1]], banks[1][:, c * 4:(c + 1) * 4], stg2[:, c * 128:(c + 1) * 128], identf[0:4, 0:4])
            V("tensor_copy", [bankR[1]], [pR], out=P5632[:, l], in_=banks[1][:, 0:4 * NUP].rearrange("p (c r) -> p c r", c=NUP))
            DQ([], [stg3R], stg3[0:1, :], g_pre_mix[l:l + 1, :])
            DQ([], [stg3R], stg3[1:2, :], g_pre_ffn[l:l + 1, :])
            for c in range(KT):
                PE("transpose", [stg3R, cR], [bankR[2]], banks[2][:, c * 2:(c + 1) * 2], stg3[:, c * 128:(c + 1) * 128], identf[0:2, 0:2])
            V("tensor_copy", [bankR[2]], [pR], out=P1024[:, l], in_=banks[2][:, 0:2 * KT].rearrange("p (c r) -> p c r", c=KT))
        S.barrier()
        AR.reset(m0)

        ypR = [Res("yp%d" % i) for i in range(SEQ // 128)]
        ysR = Res("ys")

        def rms_rstd(src_ap, rows, junk, ss, rstd, rd, wr):
            A("activation", rd, wr, out=junk[0:rows], in_=src_ap, func=AF.Square)
            V("tensor_reduce", wr, wr, out=ss[0:rows], in_=junk[0:rows], axis=AX.X, op=ALU.add)
            A("activation", wr, wr, out=rstd[0:rows], in_=ss[0:rows], func=AF.Sqrt, scale=1.0 / D, bias=EPS)
            V("reciprocal", wr, wr, out=rstd[0:rows], in_=rstd[0:rows])

        def load_w_cast(dst, src2d, resv, colchunk=1536):
            K = dst.shape[1]; N = dst.shape[2]
            for k in range(K):
                c0 = 0
                while c0 < N:
                    c1 = min(N, c0 + colchunk)
                    DW([], [resv], dst[:, k, c0:c1], src2d[k * 128:(k + 1) * 128, c0:c1])
                    c0 = c1

        for l in range(DEPTH):
            src_p = xp if l == 0 else yp
            src_s = xs if l == 0 else ys
            S.barrier()
            mM = AR.mark()
            Win = AR.alloc(BF16, [128, KT, 1536]); WinR = Res("Win")
            Wglu = AR.alloc(BF16, [128, 4, 512]); WgluR = Res("Wglu")
            Wout = AR.alloc(BF16, [128, KT, D]); WoutR = Res("Wout")
            load_w_cast(Win, w_in[l], WinR)
            load_w_cast(Wglu, w_glu[l], WgluR)
            load_w_cast(Wout, w_out[l], WoutR)
            gpost = AR.alloc(F32, [128, D]); gpostR = Res("gpost")
            DQ([], [gpostR], gpost, bass.AP(g_post_mix.tensor, g_post_mix[l].offset, [[0, 128], [1, D]]))
            PTR = AR.alloc(BF16, [128, 16, 128]); PTI = AR.alloc(BF16, [128, 16, 128])
            W3R = [AR.alloc(BF16, [128, 16, 128]) for _ in range(2)]; W3I = [AR.alloc(BF16, [128, 16, 128]) for _ in range(2)]
            Tb = AR.alloc(BF16, [128, 32, 128])
            A1_8 = AR.alloc(F32, [128, 2, 16]); A2_8 = AR.alloc(F32, [128, 2, 16])
            A1_1 = AR.alloc(F32, [128, 2, 16]); A2_1 = AR.alloc(F32, [128, 2, 16])
            COEF8 = AR.alloc(F32, [128, 2, 2, 16]); COEF16 = AR.alloc(F32, [128, 2, 2, 16]); CINV = AR.alloc(F32, [128, 2, 2, 16])
            tabR = Res("tables")
            Dg = AR.alloc(BF16, [128, 4, 31, 128]); DgR = Res("Dg")
            for c in range(4):
                for k in range(31):
                    G("tensor_scalar", [cR, pR], [DgR], out=Dg[:, c, k, :], in0=identf, scalar1=P512[:, l, c, k:k + 1],
                      scalar2=0.0, op0=ALU.mult, op1=ALU.add)

            mS = AR.mark()
            sR = Res("s5tmp")

            def T2(shape):
                return AR.alloc(F32, shape)
            LR = LRs[:, l, :]; LI = LIs[:, l, :]; LDT = LDTs[:, l, :]; DSC = DSCs[:, l, :]
            BRt = T2([128, 16, 16]); BIt = T2([128, 16, 16]); CRt = T2([128, 16, 16]); CIt = T2([128, 16, 16])
            Cnat = T2([128, 4, 128]); Cnat2 = T2([128, 4, 128])
            V("tensor_copy", [smallR], [sR], out=BRt[:, 0, 0:1], in_=LR[:, 0:1])
            for dup in range(2):
                DQ([], [sR], Cnat[:, :, dup * 64:(dup + 1) * 64], c_re[l].rearrange("(c r) p -> r c p", c=4))
                DQ([], [sR], Cnat2[:, :, dup * 64:(dup + 1) * 64], c_im[l].rearrange("(c r) p -> r c p", c=4))
            DQ([], [sR], BRt, b_re[l].rearrange("(gp a) p h -> (a p) gp h", a=2))
            DQ([], [sR], BIt, b_im[l].rearrange("(gp a) p h -> (a p) gp h", a=2))
            for (cn, ct) in ((Cnat, CRt), (Cnat2, CIt)):
                for c in range(4):
                    PE("transpose", [sR, cR], [bankR[0]], banks[0][:, c * 128:(c + 1) * 128], cn[:, c, :], identf)
                for c in range(4):
                    pv = banks[0][:, c * 128:(c + 1) * 128].rearrange("p (g a h) -> p g a h", g=4, a=2, h=16)
                    V("tensor_copy", [bankR[0]], [sR], out=ct[0:64, 4 * c:4 * c + 4, :], in_=pv[0:64, :, 0, :])
                    V("tensor_copy", [bankR[0]], [sR], out=ct[64:128, 4 * c:4 * c + 4, :], in_=pv[64:128, :, 1, :])
            DT = T2([128, 16]); TH = T2([128, 16]); MA = T2([128, 16]); KF = T2([128, 16]); KI = AR.alloc(I32, [128, 16])
            S4 = T2([128, 16]); S2 = T2([128, 16]); C2 = T2([128, 16]); SIN = T2([128, 16]); COS = T2([128, 16])
            MAG = T2([128, 16]); MAGI = T2([128, 16]); t1 = T2([128, 16]); t2 = T2([128, 16])
            CR_ = T2([128, 16]); CI_ = T2([128, 16]); DEN = T2([128, 16]); AM1 = T2([128, 16])
            APR = T2([128, 16, 8]); API = T2([128, 16, 8]); ANR = T2([128, 16, 8]); ANI = T2([128, 16, 8])
            SPR = T2([128, 16, 8]); SPI = T2([128, 16, 8])

            def vv(out, a, b, op):
                V("tensor_tensor", [sR, bankR[0]], [sR], out=out, in0=a, in1=b, op=op)

            def vs(out, a, s1, op0, s2=None, op1=None):
                if op1 is None:
                    V("tensor_scalar", [sR], [sR], out=out, in0=a, scalar1=s1, scalar2=None, op0=op0)
                else:
                    V("tensor_scalar", [sR], [sR], out=out, in0=a, scalar1=s1, scalar2=s2, op0=op0, op1=op1)

            A("activation", [sR], [sR], out=DT, in_=LDT, func=AF.Exp)
            vv(TH, LI, DT, ALU.mult)
            vv(MA, LR, DT, ALU.mult)
            vs(KF, TH, 1.0 / TWO_PI, ALU.mult)
            V("tensor_copy", [sR], [sR], out=KI, in_=KF)
            V("tensor_copy", [sR], [sR], out=KF, in_=KI)
            V("scalar_tensor_tensor", [sR], [sR], out=TH, in0=KF, scalar=-TWO_PI, in1=TH, op0=ALU.mult, op1=ALU.add)
            A("activation", [sR], [sR], out=S4, in_=TH, func=AF.Sin, scale=0.25)
            A("activation", [sR], [sR], out=S2, in_=TH, func=AF.Sin, scale=0.5)
            vv(C2, S4, S4, ALU.mult); vs(C2, C2, -2.0, ALU.mult, 1.0, ALU.add)
            vv(SIN, S2, C2, ALU.mult); vs(SIN, SIN, 2.0, ALU.mult)
            vv(COS, S2, S2, ALU.mult); vs(COS, COS, -2.0, ALU.mult, 1.0, ALU.add)
            A("activation", [sR], [sR], out=MAG, in_=MA, func=AF.Exp)
            A("activation", [sR], [sR], out=MAGI, in_=MA, func=AF.Exp, scale=-1.0)
            vv(APR[:, :, 0], MAG, COS, ALU.mult); vv(API[:, :, 0], MAG, SIN, ALU.mult)
            vv(ANR[:, :, 0], MAGI, COS, ALU.mult); vv(ANI[:, :, 0], MAGI, SIN, ALU.mult)
            vs(ANI[:, :, 0], ANI[:, :, 0], -1.0, ALU.mult)
            vv(DEN, LR, LR, ALU.mult); vv(t1, LI, LI, ALU.mult); vv(DEN, DEN, t1, ALU.add)
            V("reciprocal", [sR], [sR], out=DEN, in_=DEN)
            vs(AM1, APR[:, :, 0], -1.0, ALU.add)
            vv(t1, AM1, LR, ALU.mult); vv(t2, API[:, :, 0], LI, ALU.mult); vv(t1, t1, t2, ALU.add); vv(CR_, t1, DEN, ALU.mult)
            vv(t1, API[:, :, 0], LR, ALU.mult); vv(t2, AM1, LI, ALU.mult); vv(t1, t1, t2, ALU.subtract); vv(CI_, t1, DEN, ALU.mult)

            TA = T2([128, 2048]); TB_ = T2([128, 2048])

            def cmul(oR, oI, aR, aI, bR, bI, shape, neg_im=False):
                n = 1
                for s in shape:
                    n *= s
                pat = {1: "p (a) -> p a", 2: "p (a b) -> p a b", 3: "p (a b c) -> p a b c"}[len(shape)]
                kw = dict(zip("abc", shape))
                ta = TA[:, 0:n].rearrange(pat, **kw); tb = TB_[:, 0:n].rearrange(pat, **kw)
                vv(ta, aR, bR, ALU.mult); vv(tb, aI, bI, ALU.mult); vv(oR, ta, tb, ALU.subtract)
                vv(ta, aR, bI, ALU.mult); vv(tb, aI, bR, ALU.mult)
                if neg_im:
                    V("scalar_tensor_tensor", [sR], [sR], out=oI, in0=ta, scalar=-1.0, in1=tb, op0=ALU.mult, op1=ALU.subtract)
                else:
                    vv(oI, ta, tb, ALU.add)

            for (XR_, XI_) in ((APR, API), (ANR, ANI)):
                cmul(XR_[:, :, 1], XI_[:, :, 1], XR_[:, :, 0], XI_[:, :, 0], XR_[:, :, 0], XI_[:, :, 0], [16])
                cmul(XR_[:, :, 2:4], XI_[:, :, 2:4], XR_[:, :, 0:2], XI_[:, :, 0:2],
                     bcast_ap(XR_[:, :, 1], 2, 2), bcast_ap(XI_[:, :, 1], 2, 2), [16, 2])
                cmul(XR_[:, :, 4:8], XI_[:, :, 4:8], XR_[:, :, 0:4], XI_[:, :, 0:4],
                     bcast_ap(XR_[:, :, 3], 2, 4), bcast_ap(XI_[:, :, 3], 2, 4), [16, 4])
            PRt = T2([128, 16, 8, 16]); PIt = T2([128, 16, 8, 16]); QRt = T2([128, 16, 8, 16]); QIt = T2([128, 16, 8, 16])
            fenceR = Res("fence"); fenceR.w = list(sR.w)
            qR = Res("qtab")
            TA2 = T2([128, 16, 8, 16]); TB2 = T2([128, 16, 8, 16])
            qa_r = bcast_ap(APR, 3, 16); qa_i = bcast_ap(API, 3, 16); qb_r = bcast_ap(CRt, 2, 8); qb_i = bcast_ap(CIt, 2, 8)

            def gg(out, a, b, op):
                G("tensor_tensor", [fenceR, qR], [qR], out=out, in0=a, in1=b, op=op)
            gg(TA2, qa_r, qb_r, ALU.mult); gg(TB2, qa_i, qb_i, ALU.mult); gg(QRt, TA2, TB2, ALU.subtract)
            gg(TA2, qa_r, qb_i, ALU.mult); gg(TB2, qa_i, qb_r, ALU.mult)
            G("tensor_scalar", [qR], [qR], out=TA2, in0=TA2, scalar1=-1.0, scalar2=0.0, op0=ALU.mult, op1=ALU.add)
            gg(QIt, TA2, TB2, ALU.subtract)
            cmul(SPR, SPI, ANR, ANI, bcast_ap(CR_, 2, 8), bcast_ap(CI_, 2, 8), [16, 8])
            cmul(PRt, PIt, bcast_ap(SPR, 3, 16), bcast_ap(SPI, 3, 16), bcast_ap(BRt, 2, 8), bcast_ap(BIt, 2, 8), [16, 8, 16])
            PR2 = PRt.rearrange("p g m h -> p g (m h)"); PI2 = PIt.rearrange("p g m h -> p g (m h)")
            QR2 = QRt.rearrange("p g m h -> p g (m h)"); QI2 = QIt.rearrange("p g m h -> p g (m h)")
            for a in range(2):
                sl = slice(a * 64, (a + 1) * 64)
                V("memset", [], [tabR], W3R[a], 0.0)
                V("memset", [], [tabR], W3I[a], 0.0)
                V("tensor_copy", [qR, tabR], [tabR], out=W3R[a][sl], in_=QR2[sl])
                V("tensor_copy", [qR, tabR], [tabR], out=W3I[a][sl], in_=QI2[sl])
            for (A1, A2, m) in ((A1_8, A2_8, 7), (A1_1, A2_1, 0)):
                V("tensor_copy", [sR], [tabR], out=A1[:, 0, :], in_=APR[:, :, m])
                V("tensor_copy", [sR], [tabR], out=A1[:, 1, :], in_=APR[:, :, m])
                V("tensor_copy", [sR], [tabR], out=A2[:, 0, :], in_=API[:, :, m])
                V("tensor_scalar", [sR], [tabR], out=A2[:, 1, :], in0=API[:, :, m], scalar1=-1.0, scalar2=None, op0=ALU.mult)
            V("tensor_copy", [tabR], [tabR], out=COEF8[:, 0], in_=A1_8)
            V("tensor_copy", [tabR], [tabR], out=COEF8[:, 1], in_=A2_8)
            L16R = T2([128, 16]); L16I = T2([128, 16])
            cmul(L16R, L16I, APR[:, :, 7], API[:, :, 7], APR[:, :, 7], API[:, :, 7], [16])
            for (CF, xr, xi) in ((COEF16, L16R, L16I), (CINV, ANR[:, :, 7], ANI[:, :, 7])):
                V("tensor_copy", [sR], [tabR], out=CF[:, 0, 0, :], in_=xr)
                V("tensor_copy", [sR], [tabR], out=CF[:, 0, 1, :], in_=xr)
                V("tensor_copy", [sR], [tabR], out=CF[:, 1, 0, :], in_=xi)
                V("tensor_scalar", [sR], [tabR], out=CF[:, 1, 1, :], in0=xi, scalar1=-1.0, scalar2=None, op0=ALU.mult)
            for (src, dst) in ((PR2, PTR), (PI2, PTI)):
                for q in range(4):
                    for i in range(4):
                        PE("transpose", [sR, cR], [bankR[1]], banks[1][:, i * 128:(i + 1) * 128], src[:, 4 * q + i, :], identf)
                    V("tensor_copy", [bankR[1]], [tabR], out=dst[:, 4 * q:4 * q + 4, :],
                      in_=banks[1][:, :].rearrange("p (i c) -> p i c", i=4))
            Ttmps = [T2([128, 4, 128]) for _ in range(2)]; ttR = [Res("Ttmp0"), Res("Ttmp1")]
            rdR = Res("pq_ro"); rdR.w = list(sR.w) + list(qR.w)
            for r in range(4):
                b0 = 2 + 2 * (r % 2)
                for a in range(2):
                    sl = slice(a * 64, (a + 1) * 64)
                    for i in range(4):
                        gp = 4 * r + i
                        PE("matmul", [rdR], [bankR[b0 + a]], banks[b0 + a][:, i * 128:(i + 1) * 128], lhsT=PR2[sl, gp, :], rhs=QR2[sl, gp, :],
                           start=True, stop=False, signal=False)
                        PE("matmul", [rdR], [bankR[b0 + a]], banks[b0 + a][:, i * 128:(i + 1) * 128], lhsT=PI2[sl, gp, :], rhs=QI2[sl, gp, :],
                           start=False, stop=True)
                for a in range(2):
                    Ttmp = Ttmps[a]
                    V("tensor_tensor", [bankR[b0 + a], cR], [ttR[a]], out=Ttmp, in0=banks[b0 + a][:, :].rearrange("p (i c) -> p i c", i=4),
                      in1=bcast_ap(maskf, 1, 4), op=ALU.mult)
                    for i in range(4):
                        g = 2 * (4 * r + i) + a
                        V("scalar_tensor_tensor", [ttR[a], cR, smallR], [tabR], out=Tb[:, g, :], in0=identf, scalar=DSC[:, g:g + 1],
                          in1=Ttmp[:, i, :], op0=ALU.mult, op1=ALU.add)
            S.barrier()
            AR.reset(mS)

            NT = 256; NB = 32
            NTILES = SEQ // NT
            hT = AR.alloc(BF16, [128, KT, NT]); hTR = Res("hT")
            mixTs = []; mix_offs = []
            for _ in range(2):
                mixTs.append(AR.alloc(BF16, [128, KT, NT])); mix_offs.append(AR.last_off)
            mixRs = [Res("mixT0"), Res("mixT1")]
            XK1 = AR.alloc(F32, [128, 2, D]); XK1R = [Res("XK_0"), Res("XK_1")]
            XL = AR.alloc(F32, [128, 2, D]); XLR = [Res("XL_0"), Res("XL_1")]
            hb2 = AR.alloc(BF16, [128, 2, D]); hbRs = [Res("hb0"), Res("hb1")]
            junk = AR.alloc(F32, [128, D]); junkR = Res("junk")
            TMPX = junk; tmpxR = junkR
            ss = AR.alloc(F32, [128, 1]); rstd = AR.alloc(F32, [128, 1]); stR = Res("stat")
            ssf = AR.alloc(F32, [128, 2]); rstdf = AR.alloc(F32, [128, 2]); stfR = [Res("statf0"), Res("statf1")]
            Uflats = []; u_offs = []
            for _ in range(2):
                Uflats.append(AR.alloc(BF16, [NB, 4096])); u_offs.append(AR.last_off)
            UtokRs = [Res("Utok0"), Res("Utok1")]
            Utok4s = [u.rearrange("n (g j h) -> n g j h", g=32, j=8, h=16) for u in Uflats]
            atok5s = [u.rearrange("n (c j g h) -> n c j g h", c=4, j=8, g=8, h=16) for u in Uflats]
            Ublks = [AR.alloc(BF16, [128, 32, NB]) for _ in range(2)]; UblkRs = [Res("Ublk0"), Res("Ublk1")]
            XSb = []; xs_offs = []
            for _ in range(2):
                XSb.append(AR.alloc(F32, [128, NB, 2, 16])); xs_offs.append(AR.last_off)
            XSbR = [Res("XS0"), Res("XS1")]
            VBs = [AR.alloc(F32, [128, NB // 2, 2, 16]) for _ in range(2)]; vbRs = [Res("VB0"), Res("VB1")]
            TT = AR.alloc(F32, [128, 2, 16]); P12 = AR.alloc(F32, [128, 2, 2, 16]); scanR = Res("scan")
            Hb = AR.alloc(BF16, [128, 2, 16, NB]); HbR = Res("Hb")
            aT = AR.alloc(BF16, [128, 4, NT]); aTR = Res("aT")
            MG1 = AR.alloc(F32, [128, D]); mgR = [Res("MG0"), Res("MG1")]
            sse = AR.alloc(F32, [128, 2]); rstde = AR.alloc(F32, [128, 2]); steR = [Res("ste0"), Res("ste1")]
            ZFt = AR.alloc(F32, [128, 4, 32]); ZFtR = Res("ZFt")
            zb = AR.alloc(BF16, [128, 4, 30 + NT]); zbR = Res("zb")
            CC = AR.alloc(F32, [128, 4, NT]); CCR = Res("CC")
            MEAN = AR.alloc(F32, [128, NT]); RS = AR.alloc(F32, [128, NT]); lnR = Res("ln")
            LT = AR.alloc(F32, [128, NT]); ltR = Res("lt")
            SG = MEAN; SGR = lnR
            CSQ = LT; CSQR = ltR
            RED = AR.alloc(F32, [128, NS]); redR = Res("red")
            ZTOK = AR.alloc(F32, [30, 512]); ztokR = Res("ztok")
            H0tok = AR.alias(u_offs[1], F32, [128, 2048]); H0tokR = UtokRs[1]
            STO = H0tok; STOR = H0tokR
            bufT = H0tok[:, 0:1920].rearrange("p (c t k) -> p c t k", c=4, t=NS, k=30); bufTR = H0tokR
            H0 = AR.alias(mix_offs[1], F32, [128, NS, 2, 16]); H0R = mixRs[1]
            XSs = AR.alias(mix_offs[1] + 2048, F32, [128, NS, 2, 16]); XSsR = mixRs[1]
            TTs = AR.alias(xs_offs[1], F32, [128, NS, 2, 16])
            bufS = AR.alias(xs_offs[1] + 2048, F32, [120, 512]); bufSR = XSbR[1]
            P1s = junk[:, 0:512].rearrange("p (t r g) -> p t r g", t=NS, r=2, g=16)
            P2s = junk[:, 512:1024].rearrange("p (t r g) -> p t r g", t=NS, r=2, g=16)

            V("memset", [], [scR], SC, 0.0)
            V("memset", [], [zbR], zb, 0.0)
            tpv = bfv(banks[0])
            fl = lambda ap: ap.rearrange("p m r g -> p m (r g)")
            pairsM = [(b67, [b67R]), (b45, [bankR[4], bankR[5]])]

            def tp_(tile):
                samp = (tile == NTILES)
                par = 0 if samp else tile % 2
                return dict(samp=samp, last=(tile == NTILES - 1), nt=(NS if samp else NT), nb=(NS if samp else NB),
                            nsub=(1 if samp else NT // 128), rows=(NS if samp else 128), par=par,
                            XK=XK1, XKR=XK1R, mixT=mixTs[par], mixR=mixRs[par],
                            Utok4=Utok4s[par], atok5=atok5s[par], UtokR=UtokRs[par], Ublk=Ublks[par], UblkR=UblkRs[par],
                            XS=XSb[par], XSR=XSbR[par], VB=VBs[par], vbR=vbRs[par])

            def src_of(tile, sub):
                if tile == NTILES:
                    return src_s, ysR
                r0 = tile * NT + sub * 128
                return src_p[r0:r0 + 128, :], ypR[r0 // 128]

            def front_a_steps(tile):
                p = tp_(tile); rows = p["rows"]
                steps = []
                for sub in range(p["nsub"]):
                    sap, rres = src_of(tile, sub)
                    steps.append(lambda sub=sub, sap=sap, rres=rres: DQ([rres], [XLR[sub]], XL[0:rows, sub, :], sap))
                for sub in range(p["nsub"]):
                    steps.append(lambda sub=sub: A("activation", [XLR[sub]], [junkR], out=junk[0:rows], in_=XL[0:rows, sub, :], func=AF.Square))
                    steps.append(lambda sub=sub: V("tensor_reduce", [junkR], [stfR[sub]], out=ssf[0:rows, sub:sub + 1], in_=junk[0:rows], axis=AX.X, op=ALU.add))
                    steps.append(lambda sub=sub: A("activation", [stfR[sub]], [stfR[sub]], out=rstdf[0:rows, sub:sub + 1], in_=ssf[0:rows, sub:sub + 1],
                                                   func=AF.Sqrt, scale=1.0 / D, bias=EPS))
                    steps.append(lambda sub=sub: V("reciprocal", [stfR[sub]], [stfR[sub]], out=rstdf[0:rows, sub:sub + 1], in_=rstdf[0:rows, sub:sub + 1]))
                    steps.append(lambda sub=sub: V("tensor_scalar", [XLR[sub], stfR[sub]], [hbRs[sub]], out=hb2[0:rows, sub, :], in0=XL[0:rows, sub, :],
                                                   scalar1=rstdf[0:rows, sub:sub + 1], scalar2=None, op0=ALU.mult))
                return steps

            def front_a(tile):
                for st in front_a_steps(tile):
                    st()

            def xk_load(tile):
                p = tp_(tile); rows = p["rows"]
                for sub in range(p["nsub"]):
                    sap, rres = src_of(tile, sub)
                    DQ([rres], [XK1R[sub]], XK1[0:rows, sub, :], sap)

            def front_b(tile):
                p = tp_(tile); rows = p["rows"]
                for sub in range(p["nsub"]):
                    bkf = (0, 1)[sub]
                    tpf = bfv(banks[bkf])
                    for k in range(KT):
                        PE("transpose", [hbRs[sub], cR], [bankR[bkf]], tpf[:, k * 128:k * 128 + rows], hb2[0:rows, sub, k * 128:(k + 1) * 128],
                           identb[0:rows, 0:rows])
                    V("tensor_tensor", [bankR[bkf], pR], [hTR], out=hT[:, :, sub * 128:sub * 128 + rows],
                      in0=tpf[:, :].rearrange("p (k t) -> p k t", k=KT)[:, :, 0:rows],
                      in1=bcast_ap(P1024[:, l, :, 0], 2, rows), op=ALU.mult)

            def epilogue_early(tile):
                p = tp_(tile); rows = p["rows"]
                MGs = [junk, MG1]
                for sub in range(p["nsub"]):
                    pp, ppR = pairsM[sub % 2]
                    mR = [junkR, mgR[1]][sub]
                    A("activation", ppR, [mR], out=MGs[sub][0:rows], in_=pp[0:rows, :], func=AF.Square)
                    V("tensor_reduce", [mR], [steR[sub]], out=sse[0:rows, sub:sub + 1], in_=MGs[sub][0:rows], axis=AX.X, op=ALU.add)
                    V("tensor_tensor", ppR + [gpostR], [mR], out=MGs[sub][0:rows], in0=pp[0:rows, :], in1=gpost[0:rows], op=ALU.mult)

            def epilogue_steps(tile):
                p = tp_(tile); rows = p["rows"]; XK = p["XK"]; XKR = p["XKR"]
                MGs = [junk, MG1]
                steps = []
                for sub in range(p["nsub"]):
                    mR = [junkR, mgR[1]][sub]
                    steps.append(lambda sub=sub: A("activation", [steR[sub]], [steR[sub]], out=rstde[0:rows, sub:sub + 1], in_=sse[0:rows, sub:sub + 1],
                                                   func=AF.Sqrt, scale=1.0 / D, bias=EPS))
                    steps.append(lambda sub=sub: V("reciprocal", [steR[sub]], [steR[sub]], out=rstde[0:rows, sub:sub + 1], in_=rstde[0:rows, sub:sub + 1]))
                    steps.append(lambda sub=sub, mR=mR: V("scalar_tensor_tensor", [mR, steR[sub], XKR[sub]], [XKR[sub]], out=XK[0:rows, sub, :],
                                                          in0=MGs[sub][0:rows], scalar=rstde[0:rows, sub:sub + 1], in1=XK[0:rows, sub, :],
                                                          op0=ALU.mult, op1=ALU.add))
                    if p["samp"]:
                        steps.append(lambda sub=sub: DQ([XKR[sub]], [ysR], ys, XK[0:rows, sub, :]))
                    else:
                        r0 = tile * NT + sub * 128
                        steps.append(lambda sub=sub, r0=r0: DQ([XKR[sub]], [ypR[r0 // 128]], yp[r0:r0 + 128, :], XK[:, sub, :]))
                return steps

            def epilogue_M(tile):
                epilogue_early(tile)
                for st in epilogue_steps(tile):
                    st()

            def seg_uproj(tile):
                p = tp_(tile); samp = p["samp"]; nb = p["nb"]; Utok4 = p["Utok4"]; UtokR = p["UtokR"]
                if samp:
                    V("memset", [], [UtokR], Utok4[0:NS, :, 1:8, :], 0.0)
                    for k in range(KT):
                        PE("matmul", [hTR, WinR], [bankR[1]], banks[1][0:nb, :], lhsT=hT[:, k, 0:NS], rhs=Win[:, k, 0:512],
                           start=(k == 0), stop=(k == KT - 1), signal=(k == KT - 1))
                    V("tensor_copy", [bankR[1]], [UtokR], out=Utok4[0:nb, :, 0, :], in_=banks[1][0:nb, :].rearrange("n (g h) -> n g h", g=32))
                    return
                for c in range(4):
                    bk = 1 + (c % 2)
                    for k in range(KT):
                        PE("matmul", [hTR, WinR], [bankR[bk]], banks[bk][:, 0:NT], lhsT=Win[:, k, c * 128:(c + 1) * 128], rhs=hT[:, k, 0:NT],
                           start=(k == 0), stop=(k == KT - 1), signal=(k == KT - 1))
                    if c % 2 == 0:
                        V("tensor_copy", [bankR[bk]], [aTR], out=aT[:, c, :], in_=banks[bk][:, 0:NT])
                    else:
                        A("copy", [bankR[bk]], [aTR], out=aT[:, c, :], in_=banks[bk][:, 0:NT])
                for c in range(4):
                    bk = 1 + (c % 2)
                    tpc = bfv(banks[bk])
                    for j in range(8):
                        PE("transpose", [aTR, cR], [bankR[bk]], tpc[0:NB, j * 128:(j + 1) * 128], aT[:, c, j:NT:8], identb)
                    src = tpc[0:NB, 0:1024].rearrange("n (j g h) -> n j g h", j=8, g=8, h=16)
                    dst = Utok4[0:NB, 8 * c:8 * c + 8, :, :].rearrange("n g j h -> n j g h")
                    if c % 2 == 0:
                        V("tensor_copy", [bankR[bk]], [UtokR], out=dst, in_=src)
                    else:
                        A("copy", [bankR[bk]], [UtokR], out=dst, in_=src)

            def seg_ublk_x(tile):
                p = tp_(tile); samp = p["samp"]; nb = p["nb"]; Utok4 = p["Utok4"]; UtokR = p["UtokR"]
                Ublk = p["Ublk"]; UblkR = p["UblkR"]; XS = p["XS"]; XSR = p["XSR"]; VB = p["VB"]; vbR = p["vbR"]
                for g in range(32):
                    PE("transpose", [UtokR, cR], [bankR[0]], tpv[:, g * 32:g * 32 + nb], Utok4[0:nb, g].rearrange("n j h -> n (j h)"),
                       identb[0:nb, 0:nb])
                usrc_ = tpv[:, 0:1024].rearrange("p (g n) -> p g n", g=32)
                V("tensor_copy", [bankR[0]], [UblkR], out=Ublk[:, 0:16, 0:nb], in_=usrc_[:, 0:16, 0:nb])
                A("copy", [bankR[0]], [UblkR], out=Ublk[:, 16:32, 0:nb], in_=usrc_[:, 16:32, 0:nb])
                if samp:
                    for ri in range(2):
                        DQ([], [H0tokR], H0tok[0:NS, :], (sre, sim)[ri][l])
                        for gp in range(16):
                            PE("transpose", [H0tokR, cR], [bankR[3]], banks[3][:, gp * NS:(gp + 1) * NS],
                               H0tok[0:NS, gp * 128:(gp + 1) * 128], identf[0:NS, 0:NS])
                        V("tensor_copy", [bankR[3]], [H0R], out=H0[:, :, ri, :].rearrange("p t g -> p g t"),
                          in_=banks[3][:, 0:16 * NS].rearrange("p (g t) -> p g t", g=16))
                xdst = XSs if samp else XS
                xdR = XSsR if samp else XSR
                for q in range(2):
                    bk = (3, 1)[q % 2]
                    for i in range(8):
                        gp = 8 * q + i
                        for a in range(2):
                            g = 2 * gp + a
                            for ri, PT_ in enumerate((PTR, PTI)):
                                PE("matmul", [UblkR, tabR], [bankR[bk]],
                                   banks[bk][a * 64:(a + 1) * 64, (i * 2 + ri) * 32:(i * 2 + ri) * 32 + nb],
                                   lhsT=PT_[:, gp, a * 64:(a + 1) * 64], rhs=Ublk[:, g, 0:nb], start=True, stop=True,
                                   signal=(i == 7 and a == 1 and ri == 1))
                    V("tensor_copy", [bankR[bk]], [xdR], out=xdst[:, 0:nb, :, 8 * q:8 * q + 8].rearrange("p n r g -> p g r n"),
                      in_=banks[bk][:, :].rearrange("p (g r n) -> p g r n", g=8, r=2)[:, :, :, 0:nb])
                if not samp:
                    XSe = XS[:, 0:NB:2]; XSo = XS[:, 1:NB:2]
                    cb = lambda C_, w: bcast_ap(C_[:, w].rearrange("p r g -> p (r g)"), 1, NB // 2)
                    G("tensor_tensor", [XSR, tabR], [vbR], out=fl(VB), in0=fl(XSo), in1=cb(CINV, 0), op=ALU.mult, nosync=True)
                    G("tensor_tensor", [XSR, tabR], [XSR], out=fl(XSo), in0=fl(XSo), in1=cb(CINV, 1), op=ALU.mult, nosync=True)
                    G("tensor_tensor", [vbR, XSR], [vbR], out=VB, in0=VB, in1=swap2(XSo, 2), op=ALU.add, nosync=True)
                    G("tensor_tensor", [vbR, XSR], [vbR], out=VB, in0=VB, in1=XSe, op=ALU.add, nosync=True)
                    for m in range(nb // 2):
                        prev = SC if m == 0 else XS[:, 2 * m - 1]
                        G("tensor_tensor", [scR, XSR, scanR, vbR], [scanR], out=TT, in0=prev, in1=VB[:, m], op=ALU.add, nosync=True)
                        G("tensor_tensor", [scanR, tabR], [scanR], out=P12, in0=COEF16, in1=bcast_ap(TT, 1, 2), op=ALU.mult, nosync=True)
                        G("tensor_tensor", [scanR], [XSR], out=XS[:, 2 * m + 1], in0=P12[:, 0], in1=swap2(P12[:, 1]), op=ALU.add, nosync=True)
                    G("tensor_tensor", [scR, XSR, vbR], [vbR], out=VB[:, 0], in0=SC, in1=XS[:, 0], op=ALU.add, nosync=True)
                    G("tensor_tensor", [XSR, vbR], [vbR], out=VB[:, 1:NB // 2], in0=XS[:, 1:NB - 1:2], in1=XS[:, 2:NB:2], op=ALU.add, nosync=True)
                    G("tensor_tensor", [vbR, tabR, XSR], [XSR], out=fl(XSe), in0=fl(VB), in1=cb(COEF8, 0), op=ALU.mult, nosync=True)
                    G("tensor_tensor", [vbR, tabR], [vbR], out=fl(VB), in0=fl(VB), in1=cb(COEF8, 1), op=ALU.mult, nosync=True)
                    G("tensor_tensor", [XSR, vbR], [XSR], out=XSe, in0=XSe, in1=swap2(VB, 2), op=ALU.add, nosync=True)
                    G("tensor_copy", [scR], [HbR], out=Hb[:, :, :, 0], in_=SC, nosync=True)
                    G("tensor_copy", [XSR], [HbR], out=Hb[:, 0, :, 1:nb].rearrange("p g n -> p n g"), in_=XS[:, 0:nb - 1, 0, :], nosync=True)
                    G("tensor_copy", [XSR], [HbR], out=Hb[:, 1, :, 1:nb].rearrange("p g n -> p n g"), in_=XS[:, 0:nb - 1, 1, :], nosync=True)
                    G("tensor_copy", [XSR], [scR], out=SC, in_=XS[:, nb - 1], nosync=True)
                    if p["last"]:
                        DQ([scR], [Res()], pre[l].rearrange("(gp a) p -> (a p) gp", a=2), SC[:, 0, :], allow_slow_non_contiguous=True)
                        DQ([scR], [Res()], pim[l].rearrange("(gp a) p -> (a p) gp", a=2), SC[:, 1, :], allow_slow_non_contiguous=True)

            def seg_vg(tile):
                p = tp_(tile); samp = p["samp"]; nt = p["nt"]
                for c in range(4):
                    for k in range(KT):
                        PE("matmul", [hTR, WinR], [bankR[1]], banks[1][:, 0:nt], lhsT=Win[:, k, 1024 + c * 128:1024 + (c + 1) * 128],
                           rhs=hT[:, k, 0:nt], start=(k == 0), stop=(k == KT - 1), signal=(k == KT - 1))
                    A("activation", [bankR[1]], [SGR], out=SG[:, 0:nt], in_=banks[1][:, 0:nt], func=AF.Sigmoid)
                    for k in range(KT):
                        PE("matmul", [hTR, WinR], [bankR[2]], banks[2][:, 0:nt], lhsT=Win[:, k, 512 + c * 128:512 + (c + 1) * 128],
                           rhs=hT[:, k, 0:nt], start=(k == 0), stop=(k == KT - 1), signal=(k == KT - 1))
                    if samp:
                        V("tensor_tensor", [bankR[2], SGR], [ZFtR], out=ZFt[:, c, 0:NS], in0=banks[2][:, 0:NS], in1=SG[:, 0:NS], op=ALU.mult)
                    else:
                        V("tensor_tensor", [bankR[2], SGR], [zbR], out=zb[:, c, 30:30 + NT], in0=banks[2][:, 0:NT], in1=SG[:, 0:NT], op=ALU.mult)
                        if p["last"]:
                            V("tensor_tensor", [bankR[2], SGR], [ZFtR], out=ZFt[:, c, 0:30], in0=banks[2][:, NT - 30:NT], in1=SG[:, NT - 30:NT], op=ALU.mult)

            def seg_conv(tile):
                p = tp_(tile); samp = p["samp"]
                if samp:
                    for rt in range(4):
                        DQ([], [bufSR], bufS, scv[l, 4 * rt:4 * rt + 4].rearrange("t k c -> (t k) c"))
                        for c in range(4):
                            PE("transpose", [bufSR, cR], [bankR[2]], banks[2][:, c * 120:(c + 1) * 120], bufS[:, c * 128:(c + 1) * 128],
                               identf[0:120, 0:120])
                        V("tensor_copy", [bankR[2]], [bufTR], out=bufT[:, :, 4 * rt:4 * rt + 4, :].rearrange("p c t k -> p c (t k)"),
                          in_=banks[2][:, 0:480].rearrange("p (c x) -> p c x", c=4))
                    for c in range(4):
                        V("tensor_tensor", [bufTR, pR], [bufTR], out=bufT[:, c], in0=bufT[:, c], in1=bcast_ap(P512[:, l, c, 0:30], 1, NS), op=ALU.mult)
                        V("tensor_reduce", [bufTR], [redR], out=RED, in_=bufT[:, c], axis=AX.X, op=ALU.add)
                        V("scalar_tensor_tensor", [ZFtR, pR, redR], [CCR], out=CC[:, c, 0:NS], in0=ZFt[:, c, 0:NS], scalar=P512[:, l, c, 30:31],
                          in1=RED, op0=ALU.mult, op1=ALU.add)
                        V("tensor_scalar", [CCR, pR], [CCR], out=CC[:, c, 0:NS], in0=CC[:, c, 0:NS], scalar1=P512[:, l, c, 31:32], scalar2=None, op0=ALU.add)
                else:
                    for c in range(4):
                        bk = 1 + (c % 2)
                        for k in range(31):
                            PE("matmul", [zbR, DgR], [bankR[bk]], banks[bk][:, 0:NT], lhsT=Dg[:, c, k, :], rhs=zb[:, c, k:k + NT],
                               start=(k == 0), stop=(k == 30), signal=(k == 30))
                        A("activation", [bankR[bk], pR], [CCR], out=CC[:, c, :], in_=banks[bk][:, 0:NT], func=AF.Identity, bias=P512[:, l, c, 31:32])
                    for c in range(4):
                        V("tensor_copy", [zbR], [zbR], out=zb[:, c, 0:30], in_=zb[:, c, NT:NT + 30])

            def seg_ln(tile):
                p = tp_(tile); nt = p["nt"]; mixT = p["mixT"]; mixR = p["mixR"]
                for c in range(4):
                    PE("matmul", [CCR, cR], [bankR[3]], banks[3][:, 0:nt], lhsT=onesf, rhs=CC[:, c, 0:nt], start=(c == 0), stop=(c == 3), signal=(c == 3))
                for c in range(4):
                    A("activation", [CCR], [CSQR], out=CSQ[:, 0:nt], in_=CC[:, c, 0:nt], func=AF.Square)
                    PE("matmul", [CSQR, cR], [bankR[2]], banks[2][:, 0:nt], lhsT=onesf, rhs=CSQ[:, 0:nt], start=(c == 0), stop=(c == 3), signal=True)
                V("tensor_scalar", [bankR[3]], [lnR], out=MEAN[:, 0:nt], in0=banks[3][:, 0:nt], scalar1=1.0 / 512, scalar2=None, op0=ALU.mult)
                V("tensor_tensor", [lnR], [ltR], out=LT[:, 0:nt], in0=MEAN[:, 0:nt], in1=MEAN[:, 0:nt], op=ALU.mult)
                V("scalar_tensor_tensor", [bankR[2], ltR], [lnR], out=RS[:, 0:nt], in0=banks[2][:, 0:nt], scalar=1.0 / 512, in1=LT[:, 0:nt],
                  op0=ALU.mult, op1=ALU.subtract)
                A("activation", [lnR], [lnR], out=RS[:, 0:nt], in_=RS[:, 0:nt], func=AF.Sqrt, bias=EPS)
                V("reciprocal", [lnR], [lnR], out=RS[:, 0:nt], in_=RS[:, 0:nt])
                for c in range(4):
                    V("tensor_tensor", [CCR, lnR], [ltR], out=LT[:, 0:nt], in0=CC[:, c, 0:nt], in1=MEAN[:, 0:nt], op=ALU.subtract)
                    V("tensor_tensor", [ltR, lnR], [ltR], out=LT[:, 0:nt], in0=LT[:, 0:nt], in1=RS[:, 0:nt], op=ALU.mult)
                    A("activation", [ltR, pR], [mixR], out=mixT[:, 4 + c, 0:nt], in_=LT[:, 0:nt], func=AF.Silu,
                      scale=P512[:, l, c, 32:33], bias=P512[:, l, c, 33:34])

            def seg_scan_tail(tile):
                p = tp_(tile); samp = p["samp"]; nb = p["nb"]; XS = p["XS"]; XSR = p["XSR"]; VB = p["VB"]; vbR = p["vbR"]
                if samp:
                    V("tensor_copy", [H0R], [HbR], out=Hb[:, :, :, 0:NS].rearrange("p r g t -> p t r g"), in_=H0)
                    V("tensor_tensor", [H0R, XSsR], [scanR, XSbR[1]], out=TTs, in0=H0, in1=XSs, op=ALU.add)
                    V("tensor_tensor", [scanR, tabR, XSbR[1]], [scanR, junkR], out=P1s.rearrange("p t r g -> p t (r g)"),
                      in0=TTs.rearrange("p t r g -> p t (r g)"), in1=bcast_ap(A1_1.rearrange("p r g -> p (r g)"), 1, NS), op=ALU.mult)
                    V("tensor_tensor", [scanR, tabR, XSbR[1]], [scanR, junkR], out=P2s.rearrange("p t r g -> p t (r g)"),
                      in0=TTs.rearrange("p t r g -> p t (r g)"), in1=bcast_ap(A2_1.rearrange("p r g -> p (r g)"), 1, NS), op=ALU.mult)
                    V("tensor_tensor", [scanR, junkR], [XSsR], out=XSs[:, :, 0, :], in0=P1s[:, :, 0, :], in1=P2s[:, :, 1, :], op=ALU.add)
                    V("tensor_tensor", [scanR, junkR], [XSsR], out=XSs[:, :, 1, :], in0=P1s[:, :, 1, :], in1=P2s[:, :, 0, :], op=ALU.add)
                else:
                    pass

            def seg_sample_state_out():
                for ri, dst in enumerate((sre_o, sim_o)):
                    for q in range(4):
                        bk = (0, 3)[q % 2]
                        for i in range(4):
                            gp = 4 * q + i
                            PE("transpose", [XSsR, cR], [bankR[bk]], banks[bk][0:NS, i * 128:(i + 1) * 128], XSs[:, :, ri, gp], identf)
                        V("tensor_copy", [bankR[bk], HbR, H0R], [STOR], out=STO[0:NS, q * 512:(q + 1) * 512], in_=banks[bk][0:NS, :])
                    DQ([STOR], [STOR], dst[l], STO[0:NS, :])

            def seg_ytok(tile):
                p = tp_(tile); nb = p["nb"]; Ublk = p["Ublk"]; UblkR = p["UblkR"]; atok5 = p["atok5"]; atokR = p["UtokR"]
                for q in range(8):
                    bk = (3, 2, 1)[q % 3]
                    for i in range(4):
                        g = 4 * q + i; gp = g // 2; a = g % 2
                        o = banks[bk][0:nb, i * 128:(i + 1) * 128]
                        PE("matmul", [UblkR, tabR], [bankR[bk]], o, lhsT=Ublk[:, g, 0:nb], rhs=Tb[:, g, :], start=True, stop=False, signal=False)
                        PE("matmul", [HbR, tabR], [bankR[bk]], o, lhsT=Hb[:, 0, gp, 0:nb], rhs=W3R[a][:, gp, :], start=False, stop=False, signal=False)
                        PE("matmul", [HbR, tabR], [bankR[bk]], o, lhsT=Hb[:, 1, gp, 0:nb], rhs=W3I[a][:, gp, :], start=False, stop=True,
                           signal=(i == 3))
                    gl0 = (q % 2) * 4
                    A("activation", [bankR[bk], UblkR], [atokR], out=atok5[0:nb, q // 2, :, gl0:gl0 + 4, :].rearrange("n j g h -> n g j h"),
                      in_=banks[bk][0:nb, :].rearrange("n (g j h) -> n g j h", g=4, j=8, h=16), func=AF.Gelu)

            def seg_aT_glu(tile):
                p = tp_(tile); samp = p["samp"]; nb = p["nb"]; nt = p["nt"]; atok5 = p["atok5"]; atokR = p["UtokR"]
                mixT = p["mixT"]; mixR = p["mixR"]
                nj = 1 if samp else 8
                for c in range(4):
                    for j in range(nj):
                        PE("transpose", [atokR, cR], [bankR[0]], tpv[:, (c * 8 + j) * 32:(c * 8 + j) * 32 + nb],
                           atok5[0:nb, c, j].rearrange("n g h -> n (g h)"), identb[0:nb, 0:nb])
                if samp:
                    V("tensor_copy", [bankR[0]], [aTR], out=aT[:, :, 0:NS], in_=tpv[:, 0:1024].rearrange("p (c x) -> p c x", c=4)[:, :, 0:NS])
                else:
                    asrc_ = tpv[:, 0:1024].rearrange("p (c j n) -> p c j n", c=4, j=8)
                    V("tensor_copy", [bankR[0]], [aTR], out=aT[:, 0:2, :].rearrange("p c (n j) -> p c j n", j=8), in_=asrc_[:, 0:2])
                    A("copy", [bankR[0]], [aTR], out=aT[:, 2:4, :].rearrange("p c (n j) -> p c j n", j=8), in_=asrc_[:, 2:4])
                for mc in range(4):
                    bk = 1 + (mc % 2)
                    for kc in range(4):
                        PE("matmul", [aTR, WgluR], [bankR[bk]], banks[bk][:, 0:nt], lhsT=Wglu[:, kc, mc * 128:(mc + 1) * 128], rhs=aT[:, kc, 0:nt],
                           start=(kc == 0), stop=(kc == 3), signal=(kc == 3))
                    A("activation", [bankR[bk], pR], [SGR], out=SG[:, 0:nt], in_=banks[bk][:, 0:nt], func=AF.Sigmoid, bias=P512[:, l, mc, 34:35])
                    V("tensor_tensor", [aTR, SGR], [mixR], out=mixT[:, mc, 0:nt], in0=aT[:, mc, 0:nt], in1=SG[:, 0:nt], op=ALU.mult)

            def seg_wout(tile):
                p = tp_(tile); rows = p["rows"]; mixT = p["mixT"]; mixR = p["mixR"]
                for sub in range(p["nsub"]):
                    pp, ppR = pairsM[sub % 2]
                    for hh in range(2):
                        for k in range(KT):
                            PE("matmul", [mixR, WoutR], ppR, pp[0:rows, hh * 512:(hh + 1) * 512], lhsT=mixT[:, k, sub * 128:sub * 128 + rows],
                               rhs=Wout[:, k, hh * 512:(hh + 1) * 512], start=(k == 0), stop=(k == KT - 1), signal=(hh == 1 and k == KT - 1))

            def seg_state_out(tile):
                p = tp_(tile)
                if p["samp"]:
                    for c in range(4):
                        PE("transpose", [ZFtR, cR], [bankR[0]], banks[0][0:NS, c * 128:(c + 1) * 128], ZFt[:, c, 0:NS], identf)
                    V("tensor_copy", [bankR[0]], [ztokR], out=ZTOK[0:NS, :], in_=banks[0][0:NS, :])
                    DQ([ztokR], [ztokR], scv_o[l, :, 29, :], ZTOK[0:NS, :])
                    DQ([], [Res()], scv_o[l, :, 0:29, :], scv[l, :, 1:30, :])
                elif p["last"]:
                    for c in range(4):
                        PE("transpose", [ZFtR, cR], [bankR[0]], banks[0][0:30, c * 128:(c + 1) * 128], ZFt[:, c, 0:30], identf)
                    V("tensor_copy", [bankR[0]], [ztokR], out=ZTOK[:, :], in_=banks[0][0:30, :])
                    DQ([ztokR], [ztokR], pcv[l], ZTOK[:, :])

            pending = []

            def drip(k):
                for _ in range(k):
                    if pending:
                        pending.pop(0)()

            front_a(0)
            front_b(0)
            xk_load(0)
            seg_uproj(0); front_a(1); seg_ublk_x(0); seg_vg(0); seg_conv(0); seg_ln(0)
            seg_scan_tail(0)
            for t in range(NTILES):
                if t + 1 < NTILES:
                    seg_ytok(t); drip(3)
                    front_b(t + 1); drip(3)
                    pending.extend(front_a_steps(t + 2))
                    seg_aT_glu(t); drip(2)
                    seg_uproj(t + 1); drip(4)
                    seg_ublk_x(t + 1); drip(4)
                    seg_vg(t + 1); drip(4)
                    seg_conv(t + 1); drip(4)
                    seg_ln(t + 1); drip(100)
                    seg_scan_tail(t + 1)
                    seg_state_out(t + 1)
                    xk_load(t)
                    seg_wout(t)
                    epilogue_early(t)
                    pending.extend(epilogue_steps(t))
                else:
                    seg_ytok(t); drip(100); seg_aT_glu(t); xk_load(t); seg_wout(t); epilogue_M(t)
            s_ = NTILES
            front_b(s_); seg_uproj(s_); seg_ublk_x(s_); seg_vg(s_); seg_conv(s_); seg_ln(s_); seg_scan_tail(s_)
            seg_ytok(s_); seg_aT_glu(s_); xk_load(s_); seg_wout(s_); epilogue_M(s_); seg_state_out(s_); seg_sample_state_out()
            S.barrier()
            print("phase M arena peak", AR.peak)
            AR.reset(mM)

            Wup = AR.alloc(BF16, [128, KT, 2 * DFF]); WupQ = [Res("Wup%d" % i) for i in range(8)]
            Wdn = AR.alloc(BF16, [128, 22, D]); WdnQ = [Res("Wdn%d" % i) for i in range(22)]
            dn_next = 0
            for qi in range(4):
                for half in range(2):
                    q = qi + 4 * half
                    for k in range(KT):
                        DW([], [WupQ[q]], Wup[:, k, 704 * q:704 * (q + 1)], w_up[l][k * 128:(k + 1) * 128, 704 * q:704 * (q + 1)])
                nd = 6 if qi < 3 else 4
                for c in range(dn_next, dn_next + nd):
                    DW([], [WdnQ[c]], Wdn[:, c, :], w_down[l][c * 128:(c + 1) * 128, :])
                dn_next += nd
            WupAll = WupQ
            gpost = AR.alloc(F32, [128, D]); gpostR = Res("gpost2")
            DQ([], [gpostR], gpost, bass.AP(g_post_ffn.tensor, g_post_ffn[l].offset, [[0, 128], [1, D]]))
            hT = AR.alloc(BF16, [128, KT, 258]); hTR = Res("hT2")
            V("memset", [], [hTR], hT[:, :, 0:2], 0.0)
            XKs = [AR.alloc(F32, [128, 2, D]) for _ in range(2)]
            XKRs = [[Res("XKF%d_%d" % (b, i)) for i in range(2)] for b in range(2)]
            hb2 = AR.alloc(BF16, [128, 2, D]); hbRs = [Res("hbF0"), Res("hbF1")]
            ssf = AR.alloc(F32, [128, 2]); rstdf = AR.alloc(F32, [128, 2]); stfR = [Res("statfF0"), Res("statfF1")]
            junk = AR.alloc(F32, [128, D]); junkR = Res("junk2")
            ss = AR.alloc(F32, [128, 1]); rstd = AR.alloc(F32, [128, 1]); stR = Res("stat2")
            aT2 = AR.alloc(BF16, [128, 4, 256]); aT2R = [Res("aT2_%d" % i) for i in range(4)]
            TG = [AR.alloc(F32, [128, 256]) for _ in range(2)]; TV = [AR.alloc(F32, [128, 256]) for _ in range(2)]
            tgR = [Res("TG0"), Res("TG1")]; tvR = [Res("TV0"), Res("TV1")]
            SGT = [AR.alloc(F32, [128, 256]) for _ in range(2)]; sgtR = [Res("SGT0"), Res("SGT1")]
            RAWv = [AR.alloc(F32, [128, 258]) for _ in range(2)]; rawvR = [Res("RAWv0"), Res("RAWv1")]
            TMPV = [AR.alloc(F32, [128, 256]) for _ in range(2)]; tmpvR = [Res("TMPV0"), Res("TMPV1")]
            RAWg = RAWv; rawgR = rawvR; TMPG = TMPV; tmpgR = tmpvR
            TMPX = junk; tmpxR = junkR
            FBs = AR.alloc(F32, [32, 512]); FBsR = Res("FBs")
            FB = AR.alloc(F32, [128, NUP, NS, 2]); FBR = Res("FB"); fb_off = AR.last_off
            MG1 = AR.alias(fb_off, F32, [128, D])
            sse = AR.alloc(F32, [128, 2]); rstde = AR.alloc(F32, [128, 2]); steR = [Res("steF0"), Res("steF1")]
            UPT = AR.alloc(F32, [NS + 2, 2, 512]); uptR = [Res("UPT0"), Res("UPT1")]

            def tile_paramsF(tile):
                samp = (tile == 8)
                return samp, (1 if samp else 2), (NS if samp else 128)

            def frontF_a(tile):
                samp, nsub, rows = tile_paramsF(tile)
                XK = XKs[tile % 2]; XKR = XKRs[tile % 2]
                for sub in range(nsub):
                    if samp:
                        sap = ys; rres = ysR
                    else:
                        r0 = tile * 256 + sub * 128
                        sap = yp[r0:r0 + 128, :]; rres = ypR[r0 // 128]
                    DQ([rres], [XKR[sub]], XK[0:rows, sub, :], sap)
                    A("activation", [XKR[sub]], [junkR], out=junk[0:rows], in_=XK[0:rows, sub, :], func=AF.Square)
                    V("tensor_reduce", [junkR], [stfR[sub]], out=ssf[0:rows, sub:sub + 1], in_=junk[0:rows], axis=AX.X, op=ALU.add)
                    A("activation", [stfR[sub]], [stfR[sub]], out=rstdf[0:rows, sub:sub + 1], in_=ssf[0:rows, sub:sub + 1], func=AF.Sqrt,
                      scale=1.0 / D, bias=EPS)
                    V("reciprocal", [stfR[sub]], [stfR[sub]], out=rstdf[0:rows, sub:sub + 1], in_=rstdf[0:rows, sub:sub + 1])
                    V("tensor_scalar", [XKR[sub], stfR[sub]], [hbRs[sub]], out=hb2[0:rows, sub, :], in0=XK[0:rows, sub, :],
                      scalar1=rstdf[0:rows, sub:sub + 1], scalar2=None, op0=ALU.mult)

            def frontF_b(tile):
                samp, nsub, rows = tile_paramsF(tile)
                tpv = bfv(banks[0])
                for sub in range(nsub):
                    for k in range(KT):
                        PE("transpose", [hbRs[sub], cR], [bankR[0]], tpv[:, k * 128:k * 128 + rows], hb2[0:rows, sub, k * 128:(k + 1) * 128],
                           identb[0:rows, 0:rows])
                    V("tensor_tensor", [bankR[0], pR], [hTR], out=hT[:, :, 2 + sub * 128:2 + sub * 128 + rows],
                      in0=tpv[:, :].rearrange("p (k t) -> p k t", k=KT)[:, :, 0:rows],
                      in1=bcast_ap(P1024[:, l, :, 1], 2, rows), op=ALU.mult)

            def epilogue_F_early(tile):
                samp, nsub, rows = tile_paramsF(tile)
                pairs = [(b67, [b67R]), (b45, [bankR[4], bankR[5]])]
                MGs = [junk, MG1]
                for sub in range(nsub):
                    pp, ppR = pairs[sub % 2]
                    mR = [junkR, FBR][sub]
                    A("activation", ppR, [mR], out=MGs[sub][0:rows], in_=pp[0:rows, :], func=AF.Square)
                    V("tensor_reduce", [mR], [steR[sub]], out=sse[0:rows, sub:sub + 1], in_=MGs[sub][0:rows], axis=AX.X, op=ALU.add)
                    V("tensor_tensor", ppR + [gpostR], [mR], out=MGs[sub][0:rows], in0=pp[0:rows, :], in1=gpost[0:rows], op=ALU.mult)

            def epilogue_F_steps(tile):
                samp, nsub, rows = tile_paramsF(tile)
                XK = XKs[tile % 2]; XKR = XKRs[tile % 2]
                MGs = [junk, MG1]
                steps = []
                for sub in range(nsub):
                    mR = [junkR, FBR][sub]
                    steps.append(lambda sub=sub: A("activation", [steR[sub]], [steR[sub]], out=rstde[0:rows, sub:sub + 1], in_=sse[0:rows, sub:sub + 1],
                                                   func=AF.Sqrt, scale=1.0 / D, bias=EPS))
                    steps.append(lambda sub=sub: V("reciprocal", [steR[sub]], [steR[sub]], out=rstde[0:rows, sub:sub + 1], in_=rstde[0:rows, sub:sub + 1]))
                    steps.append(lambda sub=sub, mR=mR: V("scalar_tensor_tensor", [mR, steR[sub], XKR[sub]], [XKR[sub]], out=XK[0:rows, sub, :],
                                                          in0=MGs[sub][0:rows], scalar=rstde[0:rows, sub:sub + 1], in1=XK[0:rows, sub, :],
                                                          op0=ALU.mult, op1=ALU.add))
                    if samp:
                        steps.append(lambda sub=sub: DQ([XKR[sub]], [ysR], ys, XK[0:rows, sub, :]))
                    else:
                        r0 = tile * 256 + sub * 128
                        steps.append(lambda sub=sub, r0=r0: DQ([XKR[sub]], [ypR[r0 // 128]], yp[r0:r0 + 128, :], XK[:, sub, :]))
                return steps

            def frontF_a_steps(tile):
                samp, nsub, rows = tile_paramsF(tile)
                XK = XKs[tile % 2]; XKR = XKRs[tile % 2]
                steps = []
                for sub in range(nsub):
                    if samp:
                        sap = ys; rres = ysR
                    else:
                        r0 = tile * 256 + sub * 128
                        sap = yp[r0:r0 + 128, :]; rres = ypR[r0 // 128]
                    steps.append(lambda sub=sub, sap=sap, rres=rres: DQ([rres], [XKR[sub]], XK[0:rows, sub, :], sap))
                for sub in range(nsub):
                    steps.append(lambda sub=sub: A("activation", [XKR[sub]], [junkR], out=junk[0:rows], in_=XK[0:rows, sub, :], func=AF.Square))
                    steps.append(lambda sub=sub: V("tensor_reduce", [junkR], [stfR[sub]], out=ssf[0:rows, sub:sub + 1], in_=junk[0:rows], axis=AX.X, op=ALU.add))
                    steps.append(lambda sub=sub: A("activation", [stfR[sub]], [stfR[sub]], out=rstdf[0:rows, sub:sub + 1], in_=ssf[0:rows, sub:sub + 1],
                                                   func=AF.Sqrt, scale=1.0 / D, bias=EPS))
                    steps.append(lambda sub=sub: V("reciprocal", [stfR[sub]], [stfR[sub]], out=rstdf[0:rows, sub:sub + 1], in_=rstdf[0:rows, sub:sub + 1]))
                    steps.append(lambda sub=sub: V("tensor_scalar", [XKR[sub], stfR[sub]], [hbRs[sub]], out=hb2[0:rows, sub, :], in0=XK[0:rows, sub, :],
                                                   scalar1=rstdf[0:rows, sub:sub + 1], scalar2=None, op0=ALU.mult))
                return steps

            pendF = []

            def dripF(k):
                for _ in range(k):
                    if pendF:
                        pendF.pop(0)()

            frontF_a(0)
            frontF_b(0)
            for tile in range(9):
                samp = (tile == 8)
                nt = NS if samp else 256
                nsub = 1 if samp else 2
                rows = NS if samp else 128
                XK = XKs[tile % 2]; XKR = XKRs[tile % 2]
                pairs = [(b67, [b67R]), (b45, [bankR[4], bankR[5]])]

                def down(c, nsub=nsub, rows=rows, pairs=pairs):
                    for sub in range(nsub):
                        pp, ppR = pairs[sub % 2]
                        for hh in range(2):
                            PE("matmul", [aT2R[c % 4], WdnQ[c]], ppR, pp[0:rows, hh * 512:(hh + 1) * 512], lhsT=aT2[:, c % 4, sub * 128:sub * 128 + rows],
                               rhs=Wdn[:, c, hh * 512:(hh + 1) * 512], start=(c == 0), stop=(c == 21), signal=True, skip_group_check=True)
                if samp:
                    dripF(100)

                    def fb_round(q):
                        DQ([], [FBsR], FBs[:, 0:512], sff[l].rearrange("t r c -> (t r) c")[:, q * 512:(q + 1) * 512])
                        for i in range(4):
                            PE("transpose", [FBsR, cR], [bankR[0]], banks[0][:, i * 32:(i + 1) * 32], FBs[:, i * 128:(i + 1) * 128], identf[0:32, 0:32])
                        V("tensor_copy", [bankR[0]], [FBR], out=FB[:, 4 * q:4 * q + 4].rearrange("p c t r -> p c (t r)"),
                          in_=banks[0][:, 0:128].rearrange("p (c x) -> p c x", c=4))

                    def upt_round(q):
                        bk = 1 + (q % 2)
                        for k in range(KT):
                            PE("matmul", [hTR] + WupAll, [bankR[bk]], banks[bk][0:2 + NS, :], lhsT=hT[:, k, 0:2 + NS], rhs=Wup[:, k, q * 512:(q + 1) * 512],
                               start=(k == 0), stop=(k == KT - 1), signal=(k == KT - 1))
                        V("tensor_copy", [bankR[bk]], [uptR[q % 2]], out=UPT[0:2 + NS, q % 2, :], in_=banks[bk][0:2 + NS, :])
                        DW([uptR[q % 2]], [uptR[q % 2]], sff_o[l, :, 1, q * 512:(q + 1) * 512], UPT[2:2 + NS, q % 2, :])
                        DW([uptR[q % 2]], [uptR[q % 2]], pff[l, :, q * 512:(q + 1) * 512], UPT[0:2, q % 2, :])

                    DQ([], [Res()], sff_o[l, :, 0, :], sff[l, :, 1, :])
                    fb_round(0)
                    for q in range(11):
                        if q + 1 < 11:
                            fb_round(q + 1)
                        upt_round(q)
                for c in range(22):
                    par = c % 2
                    if c == 4 and tile < 8:
                        pendF.extend(frontF_a_steps(tile + 1))
                    for (ch, bk, Tt, TtR, isval) in ((c, (1, 3)[par], TG[par], tgR[par], False), (22 + c, (2, 0)[par], TV[par], tvR[par], True)):
                        RW = RAWv if isval else RAWg; rwR = rawvR if isval else rawgR
                        TMP = TMPV if isval else TMPG; tmpR = tmpvR if isval else tmpgR
                        w0 = P5632[:, l, ch, 0:1]; w1 = P5632[:, l, ch, 1:2]; w2 = P5632[:, l, ch, 2:3]; bb = P5632[:, l, ch, 3:4]
                        ps = banks[bk]
                        if samp:
                            for k in range(KT):
                                PE("matmul", [hTR, WupQ[(ch * 128) // 704], WupQ[(ch * 128 + 127) // 704]], [bankR[bk]], ps[:, 0:NS], lhsT=Wup[:, k, ch * 128:(ch + 1) * 128], rhs=hT[:, k, 2:2 + NS],
                                   start=(k == 0), stop=(k == KT - 1), signal=(k == KT - 1))
                            A("activation", [bankR[bk], pR], [TtR], out=Tt[:, 0:NS], in_=ps[:, 0:NS], func=AF.Identity, scale=w2, bias=bb)
                            V("scalar_tensor_tensor", [FBR, pR, TtR], [TtR], out=Tt[:, 0:NS], in0=FB[:, ch, :, 1], scalar=w1, in1=Tt[:, 0:NS],
                              op0=ALU.mult, op1=ALU.add)
                            V("scalar_tensor_tensor", [FBR, pR, TtR], [TtR], out=Tt[:, 0:NS], in0=FB[:, ch, :, 0], scalar=w0, in1=Tt[:, 0:NS],
                              op0=ALU.mult, op1=ALU.add)
                        else:
                            for k in range(KT):
                                PE("matmul", [hTR, WupQ[(ch * 128) // 704], WupQ[(ch * 128 + 127) // 704]], [bankR[bk]], ps[:, 0:nt + 2], lhsT=Wup[:, k, ch * 128:(ch + 1) * 128], rhs=hT[:, k, 0:nt + 2],
                                   start=(k == 0), stop=(k == KT - 1), signal=(k == KT - 1))
                            A("activation", [bankR[bk], pR], [TtR], out=Tt[:, 0:nt], in_=ps[:, 2:nt + 2], func=AF.Identity, scale=w2, bias=bb)
                            V("scalar_tensor_tensor", [bankR[bk], pR, TtR], [TtR], out=Tt[:, 0:nt], in0=ps[:, 0:nt], scalar=w0, in1=Tt[:, 0:nt],
                              op0=ALU.mult, op1=ALU.add)
                            if isval:
                                A("copy", [bankR[bk]], [rwR[par]], out=RW[par][:, 0:nt + 2], in_=ps[:, 0:nt + 2])
                                G("tensor_scalar", [rwR[par], pR], [tmpR[par]], out=TMP[par][:, 0:nt], in0=RW[par][:, 1:nt + 1], scalar1=w1, scalar2=0.0,
                                  op0=ALU.mult, op1=ALU.add)
                                G("tensor_tensor", [tmpR[par], TtR], [TtR], out=Tt[:, 0:nt], in0=Tt[:, 0:nt], in1=TMP[par][:, 0:nt], op=ALU.add)
                            else:
                                V("scalar_tensor_tensor", [bankR[bk], pR, TtR], [TtR], out=Tt[:, 0:nt], in0=ps[:, 1:nt + 1], scalar=w1, in1=Tt[:, 0:nt],
                                  op0=ALU.mult, op1=ALU.add)
                    A("activation", [tgR[par]], [sgtR[par]], out=SGT[par][:, 0:nt], in_=TG[par][:, 0:nt], func=AF.Silu)
                    V("tensor_tensor", [sgtR[par], tvR[par]], [aT2R[c % 4]], out=aT2[:, c % 4, 0:nt], in0=SGT[par][:, 0:nt], in1=TV[par][:, 0:nt], op=ALU.mult)
                    if c < 19:
                        dripF(2 if c < 4 else 1)
                    elif c == 19:
                        dripF(100)
                    if c >= 2:
                        down(c - 2)
                    if c == 21:
                        if not samp:
                            V("tensor_copy", [hTR], [hTR], out=hT[:, :, 0:2], in_=hT[:, :, nt:nt + 2])
                        if tile < 8:
                            frontF_b(tile + 1)
                        down(20)
                        down(21)
                epilogue_F_early(tile)
                pendF.extend(epilogue_F_steps(tile))
            dripF(100)
            S.barrier()
            AR.reset(mM)

        S.barrier()
        with nc.Block() as block:
            S.emit(block)
    return nc


_NC_CACHE = {}

_WNAMES = ["g_pre_mix", "w_in", "lam_re", "lam_im", "log_dt", "b_re", "b_im", "c_re", "c_im", "d_skip", "w_glu",
           "b_glu", "conv_w", "conv_b", "ln_g", "ln_b", "w_out", "g_post_mix", "g_pre_ffn", "w_up", "ffn_conv_w",
           "ffn_conv_b", "w_down", "g_post_ffn"]


def kernel(**inputs):
    n = 8
    f = lambda a: np.ascontiguousarray(np.asarray(a, dtype=np.float32))
    W = {k: f(inputs[k]) for k in _WNAMES}
    W["c_re"] = W["c_re"].reshape(DEPTH, 512, 64)
    W["c_im"] = W["c_im"].reshape(DEPTH, 512, 64)
    ident = np.eye(128, dtype=np.float32)
    jj = np.arange(128) // 16
    mask = (jj[None, :] >= jj[:, None]).astype(np.float32)
    x_prompt = f(inputs["x_prompt"]); x_sample = f(inputs["x_sample"])
    s_re = f(inputs["state_ssm_re"]); s_im = f(inputs["state_ssm_im"])
    s_cv = f(inputs["state_conv"]); s_ff = f(inputs["state_ffn_conv"])
    in_maps = []
    for c in range(n):
        sl = slice(c * NS, (c + 1) * NS)
        m = dict(W)
        m["xp"] = x_prompt[c]
        m["xs"] = np.ascontiguousarray(x_sample[sl, 0, :])
        m["sre"] = np.ascontiguousarray(s_re[:, sl].reshape(DEPTH, NS, 2048))
        m["sim"] = np.ascontiguousarray(s_im[:, sl].reshape(DEPTH, NS, 2048))
        m["scv"] = np.ascontiguousarray(s_cv[:, sl])
        m["sff"] = np.ascontiguousarray(s_ff[:, sl])
        m["c_ident"] = ident
        m["c_mask"] = mask
        in_maps.append(m)
    if "nc" not in _NC_CACHE:
        _NC_CACHE["nc"] = build_program()
    nc = _NC_CACHE["nc"]
    res = run_bass_kernel_spmd(nc, in_maps, core_ids=list(range(n)))
    R = res.results
    y_prompt = np.stack([R[c]["yp"] for c in range(n)], axis=0)
    y_sample = np.concatenate([R[c]["ys"] for c in range(n)], axis=0)[:, None, :]
    p_re = np.stack([R[c]["pre"] for c in range(n)], axis=1)
    p_im = np.stack([R[c]["pim"] for c in range(n)], axis=1)
    p_cv = np.stack([R[c]["pcv"] for c in range(n)], axis=1)
    p_ff = np.stack([R[c]["pff"] for c in range(n)], axis=1)
    s_re_o = np.concatenate([R[c]["sre_o"].reshape(DEPTH, NS, 32, 64) for c in range(n)], axis=1)
    s_im_o = np.concatenate([R[c]["sim_o"].reshape(DEPTH, NS, 32, 64) for c in range(n)], axis=1)
    s_cv_o = np.concatenate([R[c]["scv_o"] for c in range(n)], axis=1)
    s_ff_o = np.concatenate([R[c]["sff_o"] for c in range(n)], axis=1)
    outs = (y_prompt, y_sample, p_re, p_im, p_cv, p_ff, s_re_o, s_im_o, s_cv_o, s_ff_o)
    return tuple(np.ascontiguousarray(o, dtype=np.float32) for o in outs)
```

```python
import math
from contextlib import ExitStack

import numpy as np
import concourse.bass as bass
import concourse.mybir as mybir
from concourse.bass_utils import run_bass_kernel_spmd

F32 = mybir.dt.float32
BF16 = mybir.dt.bfloat16
I32 = mybir.dt.int32
AF = mybir.ActivationFunctionType
ALU = mybir.AluOpType
AX = mybir.AxisListType

D = 1024
KT = 8
SEQ = 2048
NS = 16
DEPTH = 2
DFF = 2816
NUP = 44
EPS = 1e-6
TWO_PI = 2.0 * math.pi


class Res:
    __slots__ = ("name", "w", "r")

    def __init__(self, name="r"):
        self.name = name
        self.w = []
        self.r = []


class Stream:
    def __init__(self, name, engname, sems, step, is_dma):
        self.name = name
        self.engname = engname
        self.sems = sems
        self.step = step
        self.count = 0
        self.is_dma = is_dma

    def semval(self, c):
        if self.is_dma:
            k = len(self.sems)
            return self.sems[(c - 1) % k], 16 * ((c - 1) // k + 1)
        return self.sems[0], c


class Sched:
    def __init__(self):
        self.ops = {"tensor": [], "vector": [], "scalar": [], "gpsimd": [], "sync": []}
        self.streams = {}
        self.waited = {}
        self.waited_dma = set()

    def add_stream(self, name, engname, sems, step, is_dma):
        self.streams[name] = Stream(name, engname, sems, step, is_dma)

    def op(self, stream, fn, reads=(), writes=(), nosync_self=False, signal=True):
        st = self.streams[stream]
        eng = st.engname
        needs = {}
        needs_dma = set()

        def need(w):
            s, c = w
            if nosync_self and s == stream:
                return
            if self.streams[s].is_dma:
                needs_dma.add((s, c))
            elif needs.get(s, 0) < c:
                needs[s] = c

        for r in reads:
            for w in r.w:
                need(w)
        for r in writes:
            for w in r.w:
                if st.is_dma and w[0] == stream:
                    continue
                need(w)
            for w in r.r:
                need(w)
        if st.is_dma and signal and stream == "dq" and st.count >= len(st.sems):
            needs_dma.add((stream, st.count + 1 - len(st.sems)))
        waits = []
        for s, c in needs.items():
            key = (eng, s)
            if self.waited.get(key, 0) >= c:
                continue
            self.waited[key] = c
            waits.append(self.streams[s].semval(c))
        for (s, c) in sorted(needs_dma):
            if self.waited.get((eng, s), 0) >= c or (eng, s, c) in self.waited_dma:
                continue
            self.waited_dma.add((eng, s, c))
            waits.append(self.streams[s].semval(c))
        if signal:
            st.count += 1
            cnew = st.count
        else:
            cnew = st.count + 1
        for r in writes:
            if st.is_dma:
                r.w = [w for w in r.w if w[0] == stream] + [(stream, cnew)]
            else:
                r.w = [(stream, cnew)]
            r.r = []
        for r in reads:
            if st.is_dma:
                r.r.append((stream, cnew))
            else:
                r.r = [w for w in r.r if w[0] != stream] + [(stream, cnew)]
        if signal:
            sem, val = st.semval(cnew)
            self.ops[eng].append((waits, fn, sem, st.step))
        else:
            self.ops[eng].append((waits, fn, None, 0))

    def barrier(self):
        for eng in self.ops:
            waits = []
            for s, st in self.streams.items():
                if st.count == 0 or self.waited.get((eng, s), 0) >= st.count:
                    continue
                self.waited[(eng, s)] = st.count
                if st.is_dma:
                    k = len(st.sems)
                    for j in range(min(k, st.count)):
                        n_j = (st.count - 1 - j) // k + 1
                        waits.append((st.sems[j], 16 * n_j))
                else:
                    waits.append((st.sems[0], st.count))
            if waits:
                self.ops[eng].append((waits, None, None, 0))

    def emit(self, block):
        def mk(engname):
            def body(e):
                for waits, fn, sem, step in self.ops[engname]:
                    for (ws, wv) in waits:
                        e.wait_ge(ws, wv)
                    if fn is not None:
                        ins = fn(e)
                        if sem is not None:
                            ins.then_inc(sem, step)
            return body
        block.tensor(mk("tensor"))
        block.vector(mk("vector"))
        block.scalar(mk("scalar"))
        block.gpsimd(mk("gpsimd"))
        block.sync(mk("sync"))


class Arena:
    def __init__(self, t32, tbf, ti32, nbytes):
        self.t32, self.tbf, self.ti32 = t32, tbf, ti32
        self.nbytes = nbytes
        self.off = 0
        self.peak = 0

    def mark(self):
        return self.off

    def reset(self, m):
        self.off = m

    def alloc(self, dtype, shape):
        n = 1
        for s in shape[1:]:
            n *= s
        size = 2 if dtype == BF16 else 4
        off = self.off
        self.off += ((n * size + 63) // 64) * 64
        self.peak = max(self.peak, self.off)
        assert self.off <= self.nbytes, ("SBUF arena overflow", self.off, self.nbytes)
        self.last_off = off
        return self._view(off, dtype, shape)

    def alias(self, off, dtype, shape):
        return self._view(off, dtype, shape)

    def _view(self, off, dtype, shape):
        n = 1
        for s in shape[1:]:
            n *= s
        size = 2 if dtype == BF16 else 4
        base = {F32: self.t32, BF16: self.tbf, I32: self.ti32}[dtype]
        e0 = off // size
        ap = base[0:shape[0], e0:e0 + n]
        if len(shape) == 3:
            ap = ap.rearrange("p (a b) -> p a b", a=shape[1], b=shape[2])
        elif len(shape) == 4:
            ap = ap.rearrange("p (a b c) -> p a b c", a=shape[1], b=shape[2], c=shape[3])
        elif len(shape) == 5:
            ap = ap.rearrange("p (a b c d) -> p a b c d", a=shape[1], b=shape[2], c=shape[3], d=shape[4])
        return ap


def bcast_ap(ap, axis, n):
    dims = [list(x) for x in ap.ap]
    dims.insert(axis, [0, n])
    return bass.AP(ap.tensor, ap.offset, dims)


def swap2(ap, axis=1):
    dims = [list(x) for x in ap.ap]
    st = dims[axis][0]
    assert dims[axis][1] == 2
    dims[axis] = [-st, 2]
    return bass.AP(ap.tensor, ap.offset + st, dims)


def build_program():
    nc = bass.Bass("TRN2", target_bir_lowering=False)
    dt_in = {}

    def din(name, shape, dtype=F32):
        dt_in[name] = nc.dram_tensor(name, list(shape), dtype, kind="ExternalInput").ap()
        return dt_in[name]

    def dout(name, shape):
        return nc.dram_tensor(name, list(shape), F32, kind="ExternalOutput").ap()

    xp = din("xp", [SEQ, D]); xs = din("xs", [NS, D])
    sre = din("sre", [DEPTH, NS, 2048]); sim = din("sim", [DEPTH, NS, 2048])
    scv = din("scv", [DEPTH, NS, 30, 512]); sff = din("sff", [DEPTH, NS, 2, 2 * DFF])
    g_pre_mix = din("g_pre_mix", [DEPTH, D]); w_in = din("w_in", [DEPTH, D, 1536])
    lam_re = din("lam_re", [DEPTH, 32, 64]); lam_im = din("lam_im", [DEPTH, 32, 64])
    log_dt = din("log_dt", [DEPTH, 32])
    b_re = din("b_re", [DEPTH, 32, 64, 16]); b_im = din("b_im", [DEPTH, 32, 64, 16])
    c_re = din("c_re", [DEPTH, 512, 64]); c_im = din("c_im", [DEPTH, 512, 64])
    d_skip = din("d_skip", [DEPTH, 512]); w_glu = din("w_glu", [DEPTH, 512, 512])
    b_glu = din("b_glu", [DEPTH, 512]); conv_w = din("conv_w", [DEPTH, 31, 512])
    conv_b = din("conv_b", [DEPTH, 512]); ln_g = din("ln_g", [DEPTH, 512]); ln_b = din("ln_b", [DEPTH, 512])
    w_out = din("w_out", [DEPTH, D, D]); g_post_mix = din("g_post_mix", [DEPTH, D])
    g_pre_ffn = din("g_pre_ffn", [DEPTH, D]); w_up = din("w_up", [DEPTH, D, 2 * DFF])
    ffn_conv_w = din("ffn_conv_w", [DEPTH, 3, 2 * DFF]); ffn_conv_b = din("ffn_conv_b", [DEPTH, 2 * DFF])
    w_down = din("w_down", [DEPTH, DFF, D]); g_post_ffn = din("g_post_ffn", [DEPTH, D])
    c_ident = din("c_ident", [128, 128]); c_mask = din("c_mask", [128, 128])

    yp = dout("yp", [SEQ, D]); ys = dout("ys", [NS, D])
    pre = dout("pre", [DEPTH, 32, 64]); pim = dout("pim", [DEPTH, 32, 64])
    pcv = dout("pcv", [DEPTH, 30, 512]); pff = dout("pff", [DEPTH, 2, 2 * DFF])
    sre_o = dout("sre_o", [DEPTH, NS, 2048]); sim_o = dout("sim_o", [DEPTH, NS, 2048])
    scv_o = dout("scv_o", [DEPTH, NS, 30, 512]); sff_o = dout("sff_o", [DEPTH, NS, 2, 2 * DFF])

    ARENA_BYTES = 206 * 1024
    with ExitStack() as es:
        E = es.enter_context
        arena_t = E(nc.sbuf_tensor("arena", [128, ARENA_BYTES // 4], F32))
        AR = Arena(arena_t, arena_t.bitcast(BF16), arena_t.bitcast(I32), ARENA_BYTES)
        banks = [E(nc.psum_tensor("bank%d" % i, [128, 512], F32)) for i in range(4)]
        b45 = E(nc.psum_tensor("bank45", [128, 1024], F32))
        banks = banks + [b45[:, 0:512], b45[:, 512:1024]]
        b67 = E(nc.psum_tensor("bank67", [128, 1024], F32))
        bankR = [Res("bank%d" % i) for i in range(6)]
        b67R = Res("b67")
        S = Sched()
        for name, eng in [("pe", "tensor"), ("dve", "vector"), ("act", "scalar"), ("pool", "gpsimd")]:
            S.add_stream(name, eng, [E(nc.semaphore("s_" + name))], 1, False)
        S.add_stream("dq", "sync", [E(nc.semaphore("s_dq%d" % i)) for i in range(64)], 16, True)
        S.add_stream("dw", "gpsimd", [E(nc.semaphore("s_dw%d" % i)) for i in range(24)], 16, True)

        def OP(stream, meth, reads, writes, *a, **kw):
            nosync = kw.pop("nosync", False) or (stream == "pe")
            sig = kw.pop("signal", True)
            S.op(stream, lambda e: getattr(e, meth)(*a, **kw), reads, writes, nosync_self=nosync, signal=sig)

        def V(meth, reads, writes, *a, **kw):
            OP("dve", meth, reads, writes, *a, **kw)

        def A(meth, reads, writes, *a, **kw):
            OP("act", meth, reads, writes, *a, **kw)

        def G(meth, reads, writes, *a, **kw):
            OP("pool", meth, reads, writes, *a, **kw)

        def PE(meth, reads, writes, *a, **kw):
            OP("pe", meth, reads, writes, *a, **kw)

        def DQ(reads, writes, out, in_, **kw):
            OP("dq", "dma_start", reads, writes, out=out, in_=in_, **kw)

        def DW(reads, writes, out, in_, **kw):
            OP("dw", "dma_start", reads, writes, out=out, in_=in_, **kw)

        def bfv(bank):
            return bank.bitcast(BF16)

        identf = AR.alloc(F32, [128, 128]); identb = AR.alloc(BF16, [128, 128])
        maskf = AR.alloc(F32, [128, 128]); onesf = AR.alloc(F32, [128, 128])
        P512 = AR.alloc(F32, [128, DEPTH, 4, 36])
        P5632 = AR.alloc(F32, [128, DEPTH, NUP, 4])
        P1024 = AR.alloc(F32, [128, DEPTH, KT, 2])
        SC = AR.alloc(F32, [128, 2, 16])
        LRs = AR.alloc(F32, [128, DEPTH, 16]); LIs = AR.alloc(F32, [128, DEPTH, 16]); LDTs = AR.alloc(F32, [128, DEPTH, 16])
        DSCs = AR.alloc(F32, [128, DEPTH, 32]); smallR = Res("small")
        cR = Res("consts"); pR = Res("params"); scR = Res("SC")
        DQ([], [cR], identf, c_ident)
        DQ([], [cR], maskf, c_mask)
        V("tensor_copy", [cR], [cR], out=identb, in_=identf)
        V("memset", [], [cR], onesf, 1.0)

        for l in range(DEPTH):
            DQ([], [smallR], LRs[:, l, :], lam_re[l].rearrange("(gp a) p -> (a p) gp", a=2), allow_slow_non_contiguous=True)
            DQ([], [smallR], LIs[:, l, :], lam_im[l].rearrange("(gp a) p -> (a p) gp", a=2), allow_slow_non_contiguous=True)
            for a in range(2):
                DQ([], [smallR], LDTs[a * 64:(a + 1) * 64, l, :], bass.AP(log_dt.tensor, log_dt[l].offset + a, [[0, 64], [2, 16]]),
                   allow_slow_non_contiguous=True)
            for j in range(8):
                DQ([], [smallR], DSCs[j * 16:(j + 1) * 16, l, :], d_skip[l].rearrange("(g h) -> h g", h=16), allow_slow_non_contiguous=True)
        m0 = AR.mark()
        stg = AR.alloc(F32, [36, 512]); stgR = Res("stg")
        stg2 = AR.alloc(F32, [4, 2 * DFF]); stg2R = Res("stg2")
        stg3 = AR.alloc(F32, [2, D]); stg3R = Res("stg3")
        for l in range(DEPTH):
            DQ([], [stgR], stg[0:31, :], conv_w[l])
            DQ([], [stgR], stg[31:32, :], conv_b[l:l + 1, :])
            DQ([], [stgR], stg[32:33, :], ln_g[l:l + 1, :])
            DQ([], [stgR], stg[33:34, :], ln_b[l:l + 1, :])
            DQ([], [stgR], stg[34:35, :], b_glu[l:l + 1, :])
            DQ([], [stgR], stg[35:36, :], d_skip[l:l + 1, :])
            for c in range(4):
                PE("transpose", [stgR, cR], [bankR[0]], banks[0][:, c * 36:(c + 1) * 36], stg[:, c * 128:(c + 1) * 128], identf[0:36, 0:36])
            V("tensor_copy", [bankR[0]], [pR], out=P512[:, l], in_=banks[0][:, 0:144].rearrange("p (c r) -> p c r", c=4))
            DQ([], [stg2R], stg2[0:3, :], ffn_conv_w[l])
            DQ([], [stg2R], stg2[3:4, :], ffn_conv_b[l:l + 1, :])
            for c in range(NUP):
                PE("transpose", [stg2R, cR], [bankR[1]], banks[1][:, c * 4:(c + 1) * 4], stg2[:, c * 128:(c + 1) * 128], identf[0:4, 0:4])
            V("tensor_copy", [bankR[1]], [pR], out=P5632[:, l], in_=banks[1][:, 0:4 * NUP].rearrange("p (c r) -> p c r", c=NUP))
            DQ([], [stg3R], stg3[0:1, :], g_pre_mix[l:l + 1, :])
            DQ([], [stg3R], stg3[1:2, :], g_pre_ffn[l:l + 1, :])
            for c in range(KT):
                PE("transpose", [stg3R, cR], [bankR[2]], banks[2][:, c * 2:(c + 1) * 2], stg3[:, c * 128:(c + 1) * 128], identf[0:2, 0:2])
            V("tensor_copy", [bankR[2]], [pR], out=P1024[:, l], in_=banks[2][:, 0:2 * KT].rearrange("p (c r) -> p c r", c=KT))
        S.barrier()
        AR.reset(m0)

        ypR = [Res("yp%d" % i) for i in range(SEQ // 128)]
        ysR = Res("ys")

        def rms_rstd(src_ap, rows, junk, ss, rstd, rd, wr):
            A("activation", rd, wr, out=junk[0:rows], in_=src_ap, func=AF.Square)
            V("tensor_reduce", wr, wr, out=ss[0:rows], in_=junk[0:rows], axis=AX.X, op=ALU.add)
            A("activation", wr, wr, out=rstd[0:rows], in_=ss[0:rows], func=AF.Sqrt, scale=1.0 / D, bias=EPS)
            V("reciprocal", wr, wr, out=rstd[0:rows], in_=rstd[0:rows])

        def load_w_cast(dst, src2d, resv, colchunk=1536):
            K = dst.shape[1]; N = dst.shape[2]
            for k in range(K):
                c0 = 0
                while c0 < N:
                    c1 = min(N, c0 + colchunk)
                    DW([], [resv], dst[:, k, c0:c1], src2d[k * 128:(k + 1) * 128, c0:c1])
                    c0 = c1

        for l in range(DEPTH):
            src_p = xp if l == 0 else yp
            src_s = xs if l == 0 else ys
            S.barrier()
            mM = AR.mark()
            Win = AR.alloc(BF16, [128, KT, 1536]); WinR = Res("Win")
            Wglu = AR.alloc(BF16, [128, 4, 512]); WgluR = Res("Wglu")
            Wout = AR.alloc(BF16, [128, KT, D]); WoutR = Res("Wout")
            load_w_cast(Win, w_in[l], WinR)
            load_w_cast(Wglu, w_glu[l], WgluR)
            load_w_cast(Wout, w_out[l], WoutR)
            gpost = AR.alloc(F32, [128, D]); gpostR = Res("gpost")
            DQ([], [gpostR], gpost, bass.AP(g_post_mix.tensor, g_post_mix[l].offset, [[0, 128], [1, D]]))
            PTR = AR.alloc(BF16, [128, 16, 128]); PTI = AR.alloc(BF16, [128, 16, 128])
            W3R = [AR.alloc(BF16, [128, 16, 128]) for _ in range(2)]; W3I = [AR.alloc(BF16, [128, 16, 128]) for _ in range(2)]
            Tb = AR.alloc(BF16, [128, 32, 128])
            A1_8 = AR.alloc(F32, [128, 2, 16]); A2_8 = AR.alloc(F32, [128, 2, 16])
            A1_1 = AR.alloc(F32, [128, 2, 16]); A2_1 = AR.alloc(F32, [128, 2, 16])
            COEF8 = AR.alloc(F32, [128, 2, 2, 16]); COEF16 = AR.alloc(F32, [128, 2, 2, 16]); CINV = AR.alloc(F32, [128, 2, 2, 16])
            tabR = Res("tables")
            Dg = AR.alloc(BF16, [128, 4, 31, 128]); DgR = Res("Dg")
            for c in range(4):
                for k in range(31):
                    G("tensor_scalar", [cR, pR], [DgR], out=Dg[:, c, k, :], in0=identf, scalar1=P512[:, l, c, k:k + 1],
                      scalar2=0.0, op0=ALU.mult, op1=ALU.add)

            mS = AR.mark()
            sR = Res("s5tmp")

            def T2(shape):
                return AR.alloc(F32, shape)
            LR = LRs[:, l, :]; LI = LIs[:, l, :]; LDT = LDTs[:, l, :]; DSC = DSCs[:, l, :]
            BRt = T2([128, 16, 16]); BIt = T2([128, 16, 16]); CRt = T2([128, 16, 16]); CIt = T2([128, 16, 16])
            Cnat = T2([128, 4, 128]); Cnat2 = T2([128, 4, 128])
            V("tensor_copy", [smallR], [sR], out=BRt[:, 0, 0:1], in_=LR[:, 0:1])
            for dup in range(2):
                DQ([], [sR], Cnat[:, :, dup * 64:(dup + 1) * 64], c_re[l].rearrange("(c r) p -> r c p", c=4))
                DQ([], [sR], Cnat2[:, :, dup * 64:(dup + 1) * 64], c_im[l].rearrange("(c r) p -> r c p", c=4))
            DQ([], [sR], BRt, b_re[l].rearrange("(gp a) p h -> (a p) gp h", a=2))
            DQ([], [sR], BIt, b_im[l].rearrange("(gp a) p h -> (a p) gp h", a=2))
            for (cn, ct) in ((Cnat, CRt), (Cnat2, CIt)):
                for c in range(4):
                    PE("transpose", [sR, cR], [bankR[0]], banks[0][:, c * 128:(c + 1) * 128], cn[:, c, :], identf)
                for c in range(4):
                    pv = banks[0][:, c * 128:(c + 1) * 128].rearrange("p (g a h) -> p g a h", g=4, a=2, h=16)
                    V("tensor_copy", [bankR[0]], [sR], out=ct[0:64, 4 * c:4 * c + 4, :], in_=pv[0:64, :, 0, :])
                    V("tensor_copy", [bankR[0]], [sR], out=ct[64:128, 4 * c:4 * c + 4, :], in_=pv[64:128, :, 1, :])
            DT = T2([128, 16]); TH = T2([128, 16]); MA = T2([128, 16]); KF = T2([128, 16]); KI = AR.alloc(I32, [128, 16])
            S4 = T2([128, 16]); S2 = T2([128, 16]); C2 = T2([128, 16]); SIN = T2([128, 16]); COS = T2([128, 16])
            MAG = T2([128, 16]); MAGI = T2([128, 16]); t1 = T2([128, 16]); t2 = T2([128, 16])
            CR_ = T2([128, 16]); CI_ = T2([128, 16]); DEN = T2([128, 16]); AM1 = T2([128, 16])
            APR = T2([128, 16, 8]); API = T2([128, 16, 8]); ANR = T2([128, 16, 8]); ANI = T2([128, 16, 8])
            SPR = T2([128, 16, 8]); SPI = T2([128, 16, 8])

            def vv(out, a, b, op):
                V("tensor_tensor", [sR, bankR[0]], [sR], out=out, in0=a, in1=b, op=op)

            def vs(out, a, s1, op0, s2=None, op1=None):
                if op1 is None:
                    V("tensor_scalar", [sR], [sR], out=out, in0=a, scalar1=s1, scalar2=None, op0=op0)
                else:
                    V("tensor_scalar", [sR], [sR], out=out, in0=a, scalar1=s1, scalar2=s2, op0=op0, op1=op1)

            A("activation", [sR], [sR], out=DT, in_=LDT, func=AF.Exp)
            vv(TH, LI, DT, ALU.mult)
            vv(MA, LR, DT, ALU.mult)
            vs(KF, TH, 1.0 / TWO_PI, ALU.mult)
            V("tensor_copy", [sR], [sR], out=KI, in_=KF)
            V("tensor_copy", [sR], [sR], out=KF, in_=KI)
            V("scalar_tensor_tensor", [sR], [sR], out=TH, in0=KF, scalar=-TWO_PI, in1=TH, op0=ALU.mult, op1=ALU.add)
            A("activation", [sR], [sR], out=S4, in_=TH, func=AF.Sin, scale=0.25)
            A("activation", [sR], [sR], out=S2, in_=TH, func=AF.Sin, scale=0.5)
            vv(C2, S4, S4, ALU.mult); vs(C2, C2, -2.0, ALU.mult, 1.0, ALU.add)
            vv(SIN, S2, C2, ALU.mult); vs(SIN, SIN, 2.0, ALU.mult)
            vv(COS, S2, S2, ALU.mult); vs(COS, COS, -2.0, ALU.mult, 1.0, ALU.add)
            A("activation", [sR], [sR], out=MAG, in_=MA, func=AF.Exp)
            A("activation", [sR], [sR], out=MAGI, in_=MA, func=AF.Exp, scale=-1.0)
            vv(APR[:, :, 0], MAG, COS, ALU.mult); vv(API[:, :, 0], MAG, SIN, ALU.mult)
            vv(ANR[:, :, 0], MAGI, COS, ALU.mult); vv(ANI[:, :, 0], MAGI, SIN, ALU.mult)
            vs(ANI[:, :, 0], ANI[:, :, 0], -1.0, ALU.mult)
            vv(DEN, LR, LR, ALU.mult); vv(t1, LI, LI, ALU.mult); vv(DEN, DEN, t1, ALU.add)
            V("reciprocal", [sR], [sR], out=DEN, in_=DEN)
            vs(AM1, APR[:, :, 0], -1.0, ALU.add)
            vv(t1, AM1, LR, ALU.mult); vv(t2, API[:, :, 0], LI, ALU.mult); vv(t1, t1, t2, ALU.add); vv(CR_, t1, DEN, ALU.mult)
            vv(t1, API[:, :, 0], LR, ALU.mult); vv(t2, AM1, LI, ALU.mult); vv(t1, t1, t2, ALU.subtract); vv(CI_, t1, DEN, ALU.mult)

            TA = T2([128, 2048]); TB_ = T2([128, 2048])

            def cmul(oR, oI, aR, aI, bR, bI, shape, neg_im=False):
                n = 1
                for s in shape:
                    n *= s
                pat = {1: "p (a) -> p a", 2: "p (a b) -> p a b", 3: "p (a b c) -> p a b c"}[len(shape)]
                kw = dict(zip("abc", shape))
                ta = TA[:, 0:n].rearrange(pat, **kw); tb = TB_[:, 0:n].rearrange(pat, **kw)
                vv(ta, aR, bR, ALU.mult); vv(tb, aI, bI, ALU.mult); vv(oR, ta, tb, ALU.subtract)
                vv(ta, aR, bI, ALU.mult); vv(tb, aI, bR, ALU.mult)
                if neg_im:
                    V("scalar_tensor_tensor", [sR], [sR], out=oI, in0=ta, scalar=-1.0, in1=tb, op0=ALU.mult, op1=ALU.subtract)
                else:
                    vv(oI, ta, tb, ALU.add)

            for (XR_, XI_) in ((APR, API), (ANR, ANI)):
                cmul(XR_[:, :, 1], XI_[:, :, 1], XR_[:, :, 0], XI_[:, :, 0], XR_[:, :, 0], XI_[:, :, 0], [16])
                cmul(XR_[:, :, 2:4], XI_[:, :, 2:4], XR_[:, :, 0:2], XI_[:, :, 0:2],
                     bcast_ap(XR_[:, :, 1], 2, 2), bcast_ap(XI_[:, :, 1], 2, 2), [16, 2])
                cmul(XR_[:, :, 4:8], XI_[:, :, 4:8], XR_[:, :, 0:4], XI_[:, :, 0:4],
                     bcast_ap(XR_[:, :, 3], 2, 4), bcast_ap(XI_[:, :, 3], 2, 4), [16, 4])
            PRt = T2([128, 16, 8, 16]); PIt = T2([128, 16, 8, 16]); QRt = T2([128, 16, 8, 16]); QIt = T2([128, 16, 8, 16])
            fenceR = Res("fence"); fenceR.w = list(sR.w)
            qR = Res("qtab")
            TA2 = T2([128, 16, 8, 16]); TB2 = T2([128, 16, 8, 16])
            qa_r = bcast_ap(APR, 3, 16); qa_i = bcast_ap(API, 3, 16); qb_r = bcast_ap(CRt, 2, 8); qb_i = bcast_ap(CIt, 2, 8)

            def gg(out, a, b, op):
                G("tensor_tensor", [fenceR, qR], [qR], out=out, in0=a, in1=b, op=op)
            gg(TA2, qa_r, qb_r, ALU.mult); gg(TB2, qa_i, qb_i, ALU.mult); gg(QRt, TA2, TB2, ALU.subtract)
            gg(TA2, qa_r, qb_i, ALU.mult); gg(TB2, qa_i, qb_r, ALU.mult)
            G("tensor_scalar", [qR], [qR], out=TA2, in0=TA2, scalar1=-1.0, scalar2=0.0, op0=ALU.mult, op1=ALU.add)
            gg(QIt, TA2, TB2, ALU.subtract)
            cmul(SPR, SPI, ANR, ANI, bcast_ap(CR_, 2, 8), bcast_ap(CI_, 2, 8), [16, 8])
            cmul(PRt, PIt, bcast_ap(SPR, 3, 16), bcast_ap(SPI, 3, 16), bcast_ap(BRt, 2, 8), bcast_ap(BIt, 2, 8), [16, 8, 16])
            PR2 = PRt.rearrange("p g m h -> p g (m h)"); PI2 = PIt.rearrange("p g m h -> p g (m h)")
            QR2 = QRt.rearrange("p g m h -> p g (m h)"); QI2 = QIt.rearrange("p g m h -> p g (m h)")
            for a in range(2):
                sl = slice(a * 64, (a + 1) * 64)
                V("memset", [], [tabR], W3R[a], 0.0)
                V("memset", [], [tabR], W3I[a], 0.0)
                V("tensor_copy", [qR, tabR], [tabR], out=W3R[a][sl], in_=QR2[sl])
                V("tensor_copy", [qR, tabR], [tabR], out=W3I[a][sl], in_=QI2[sl])
            for (A1, A2, m) in ((A1_8, A2_8, 7), (A1_1, A2_1, 0)):
                V("tensor_copy", [sR], [tabR], out=A1[:, 0, :], in_=APR[:, :, m])
                V("tensor_copy", [sR], [tabR], out=A1[:, 1, :], in_=APR[:, :, m])
                V("tensor_copy", [sR], [tabR], out=A2[:, 0, :], in_=API[:, :, m])
                V("tensor_scalar", [sR], [tabR], out=A2[:, 1, :], in0=API[:, :, m], scalar1=-1.0, scalar2=None, op0=ALU.mult)
            V("tensor_copy", [tabR], [tabR], out=COEF8[:, 0], in_=A1_8)
            V("tensor_copy", [tabR], [tabR], out=COEF8[:, 1], in_=A2_8)
            L16R = T2([128, 16]); L16I = T2([128, 16])
            cmul(L16R, L16I, APR[:, :, 7], API[:, :, 7], APR[:, :, 7], API[:, :, 7], [16])
            for (CF, xr, xi) in ((COEF16, L16R, L16I), (CINV, ANR[:, :, 7], ANI[:, :, 7])):
                V("tensor_copy", [sR], [tabR], out=CF[:, 0, 0, :], in_=xr)
                V("tensor_copy", [sR], [tabR], out=CF[:, 0, 1, :], in_=xr)
                V("tensor_copy", [sR], [tabR], out=CF[:, 1, 0, :], in_=xi)
                V("tensor_scalar", [sR], [tabR], out=CF[:, 1, 1, :], in0=xi, scalar1=-1.0, scalar2=None, op0=ALU.mult)
            for (src, dst) in ((PR2, PTR), (PI2, PTI)):
                for q in range(4):
                    for i in range(4):
                        PE("transpose", [sR, cR], [bankR[1]], banks[1][:, i * 128:(i + 1) * 128], src[:, 4 * q + i, :], identf)
                    V("tensor_copy", [bankR[1]], [tabR], out=dst[:, 4 * q:4 * q + 4, :],
                      in_=banks[1][:, :].rearrange("p (i c) -> p i c", i=4))
            Ttmps = [T2([128, 4, 128]) for _ in range(2)]; ttR = [Res("Ttmp0"), Res("Ttmp1")]
            rdR = Res("pq_ro"); rdR.w = list(sR.w) + list(qR.w)
            for r in range(4):
                b0 = 2 + 2 * (r % 2)
                for a in range(2):
                    sl = slice(a * 64, (a + 1) * 64)
                    for i in range(4):
                        gp = 4 * r + i
                        PE("matmul", [rdR], [bankR[b0 + a]], banks[b0 + a][:, i * 128:(i + 1) * 128], lhsT=PR2[sl, gp, :], rhs=QR2[sl, gp, :],
                           start=True, stop=False, signal=False)
                        PE("matmul", [rdR], [bankR[b0 + a]], banks[b0 + a][:, i * 128:(i + 1) * 128], lhsT=PI2[sl, gp, :], rhs=QI2[sl, gp, :],
                           start=False, stop=True)
                for a in range(2):
                    Ttmp = Ttmps[a]
                    V("tensor_tensor", [bankR[b0 + a], cR], [ttR[a]], out=Ttmp, in0=banks[b0 + a][:, :].rearrange("p (i c) -> p i c", i=4),
                      in1=bcast_ap(maskf, 1, 4), op=ALU.mult)
                    for i in range(4):
                        g = 2 * (4 * r + i) + a
                        V("scalar_tensor_tensor", [ttR[a], cR, smallR], [tabR], out=Tb[:, g, :], in0=identf, scalar=DSC[:, g:g + 1],
                          in1=Ttmp[:, i, :], op0=ALU.mult, op1=ALU.add)
            S.barrier()
            AR.reset(mS)

            NT = 256; NB = 32
            NTILES = SEQ // NT
            hT = AR.alloc(BF16, [128, KT, NT]); hTR = Res("hT")
            mixTs = []; mix_offs = []
            for _ in range(2):
                mixTs.append(AR.alloc(BF16, [128, KT, NT])); mix_offs.append(AR.last_off)
            mixRs = [Res("mixT0"), Res("mixT1")]
            XK1 = AR.alloc(F32, [128, 2, D]); XK1R = [Res("XK_0"), Res("XK_1")]
            XL = AR.alloc(F32, [128, 2, D]); XLR = [Res("XL_0"), Res("XL_1")]
            hb2 = AR.alloc(BF16, [128, 2, D]); hbRs = [Res("hb0"), Res("hb1")]
            junk = AR.alloc(F32, [128, D]); junkR = Res("junk")
            TMPX = junk; tmpxR = junkR
            ss = AR.alloc(F32, [128, 1]); rstd = AR.alloc(F32, [128, 1]); stR = Res("stat")
            ssf = AR.alloc(F32, [128, 2]); rstdf = AR.alloc(F32, [128, 2]); stfR = [Res("statf0"), Res("statf1")]
            Uflats = []; u_offs = []
            for _ in range(2):
                Uflats.append(AR.alloc(BF16, [NB, 4096])); u_offs.append(AR.last_off)
            UtokRs = [Res("Utok0"), Res("Utok1")]
            Utok4s = [u.rearrange("n (g j h) -> n g j h", g=32, j=8, h=16) for u in Uflats]
            atok5s = [u.rearrange("n (c j g h) -> n c j g h", c=4, j=8, g=8, h=16) for u in Uflats]
            Ublks = [AR.alloc(BF16, [128, 32, NB]) for _ in range(2)]; UblkRs = [Res("Ublk0"), Res("Ublk1")]
            XSb = []; xs_offs = []
            for _ in range(2):
                XSb.append(AR.alloc(F32, [128, NB, 2, 16])); xs_offs.append(AR.last_off)
            XSbR = [Res("XS0"), Res("XS1")]
            VBs = [AR.alloc(F32, [128, NB // 2, 2, 16]) for _ in range(2)]; vbRs = [Res("VB0"), Res("VB1")]
            TT = AR.alloc(F32, [128, 2, 16]); P12 = AR.alloc(F32, [128, 2, 2, 16]); scanR = Res("scan")
            Hb = AR.alloc(BF16, [128, 2, 16, NB]); HbR = Res("Hb")
            aT = AR.alloc(BF16, [128, 4, NT]); aTR = Res("aT")
            MG1 = AR.alloc(F32, [128, D]); mgR = [Res("MG0"), Res("MG1")]
            sse = AR.alloc(F32, [128, 2]); rstde = AR.alloc(F32, [128, 2]); steR = [Res("ste0"), Res("ste1")]
            ZFt = AR.alloc(F32, [128, 4, 32]); ZFtR = Res("ZFt")
            zb = AR.alloc(BF16, [128, 4, 30 + NT]); zbR = Res("zb")
            CC = AR.alloc(F32, [128, 4, NT]); CCR = Res("CC")
            MEAN = AR.alloc(F32, [128, NT]); RS = AR.alloc(F32, [128, NT]); lnR = Res("ln")
            LT = AR.alloc(F32, [128, NT]); ltR = Res("lt")
            SG = MEAN; SGR = lnR
            CSQ = LT; CSQR = ltR
            RED = AR.alloc(F32, [128, NS]); redR = Res("red")
            ZTOK = AR.alloc(F32, [30, 512]); ztokR = Res("ztok")
            H0tok = AR.alias(u_offs[1], F32, [128, 2048]); H0tokR = UtokRs[1]
            STO = H0tok; STOR = H0tokR
            bufT = H0tok[:, 0:1920].rearrange("p (c t k) -> p c t k", c=4, t=NS, k=30); bufTR = H0tokR
            H0 = AR.alias(mix_offs[1], F32, [128, NS, 2, 16]); H0R = mixRs[1]
            XSs = AR.alias(mix_offs[1] + 2048, F32, [128, NS, 2, 16]); XSsR = mixRs[1]
            TTs = AR.alias(xs_offs[1], F32, [128, NS, 2, 16])
            bufS = AR.alias(xs_offs[1] + 2048, F32, [120, 512]); bufSR = XSbR[1]
            P1s = junk[:, 0:512].rearrange("p (t r g) -> p t r g", t=NS, r=2, g=16)
            P2s = junk[:, 512:1024].rearrange("p (t r g) -> p t r g", t=NS, r=2, g=16)

            V("memset", [], [scR], SC, 0.0)
            V("memset", [], [zbR], zb, 0.0)
            tpv = bfv(banks[0])
            fl = lambda ap: ap.rearrange("p m r g -> p m (r g)")
            pairsM = [(b67, [b67R]), (b45, [bankR[4], bankR[5]])]

            def tp_(tile):
                samp = (tile == NTILES)
                par = 0 if samp else tile % 2
                return dict(samp=samp, last=(tile == NTILES - 1), nt=(NS if samp else NT), nb=(NS if samp else NB),
                            nsub=(1 if samp else NT // 128), rows=(NS if samp else 128), par=par,
                            XK=XK1, XKR=XK1R, mixT=mixTs[par], mixR=mixRs[par],
                            Utok4=Utok4s[par], atok5=atok5s[par], UtokR=UtokRs[par], Ublk=Ublks[par], UblkR=UblkRs[par],
                            XS=XSb[par], XSR=XSbR[par], VB=VBs[par], vbR=vbRs[par])

            def src_of(tile, sub):
                if tile == NTILES:
                    return src_s, ysR
                r0 = tile * NT + sub * 128
                return src_p[r0:r0 + 128, :], ypR[r0 // 128]

            def front_a_steps(tile):
                p = tp_(tile); rows = p["rows"]
                steps = []
                for sub in range(p["nsub"]):
                    sap, rres = src_of(tile, sub)
                    steps.append(lambda sub=sub, sap=sap, rres=rres: DQ([rres], [XLR[sub]], XL[0:rows, sub, :], sap))
                for sub in range(p["nsub"]):
                    steps.append(lambda sub=sub: A("activation", [XLR[sub]], [junkR], out=junk[0:rows], in_=XL[0:rows, sub, :], func=AF.Square))
                    steps.append(lambda sub=sub: V("tensor_reduce", [junkR], [stfR[sub]], out=ssf[0:rows, sub:sub + 1], in_=junk[0:rows], axis=AX.X, op=ALU.add))
                    steps.append(lambda sub=sub: A("activation", [stfR[sub]], [stfR[sub]], out=rstdf[0:rows, sub:sub + 1], in_=ssf[0:rows, sub:sub + 1],
                                                   func=AF.Sqrt, scale=1.0 / D, bias=EPS))
                    steps.append(lambda sub=sub: V("reciprocal", [stfR[sub]], [stfR[sub]], out=rstdf[0:rows, sub:sub + 1], in_=rstdf[0:rows, sub:sub + 1]))
                    steps.append(lambda sub=sub: V("tensor_scalar", [XLR[sub], stfR[sub]], [hbRs[sub]], out=hb2[0:rows, sub, :], in0=XL[0:rows, sub, :],
                                                   scalar1=rstdf[0:rows, sub:sub + 1], scalar2=None, op0=ALU.mult))
                return steps

            def front_a(tile):
                for st in front_a_steps(tile):
                    st()

            def xk_load(tile):
                p = tp_(tile); rows = p["rows"]
                for sub in range(p["nsub"]):
                    sap, rres = src_of(tile, sub)
                    DQ([rres], [XK1R[sub]], XK1[0:rows, sub, :], sap)

            def front_b(tile):
                p = tp_(tile); rows = p["rows"]
                for sub in range(p["nsub"]):
                    bkf = (0, 1)[sub]
                    tpf = bfv(banks[bkf])
                    for k in range(KT):
                        PE("transpose", [hbRs[sub], cR], [bankR[bkf]], tpf[:, k * 128:k * 128 + rows], hb2[0:rows, sub, k * 128:(k + 1) * 128],
                           identb[0:rows, 0:rows])
                    V("tensor_tensor", [bankR[bkf], pR], [hTR], out=hT[:, :, sub * 128:sub * 128 + rows],
                      in0=tpf[:, :].rearrange("p (k t) -> p k t", k=KT)[:, :, 0:rows],
                      in1=bcast_ap(P1024[:, l, :, 0], 2, rows), op=ALU.mult)

            def epilogue_early(tile):
                p = tp_(tile); rows = p["rows"]
                MGs = [junk, MG1]
                for sub in range(p["nsub"]):
                    pp, ppR = pairsM[sub % 2]
                    mR = [junkR, mgR[1]][sub]
                    A("activation", ppR, [mR], out=MGs[sub][0:rows], in_=pp[0:rows, :], func=AF.Square)
                    V("tensor_reduce", [mR], [steR[sub]], out=sse[0:rows, sub:sub + 1], in_=MGs[sub][0:rows], axis=AX.X, op=ALU.add)
                    V("tensor_tensor", ppR + [gpostR], [mR], out=MGs[sub][0:rows], in0=pp[0:rows, :], in1=gpost[0:rows], op=ALU.mult)

            def epilogue_steps(tile):
                p = tp_(tile); rows = p["rows"]; XK = p["XK"]; XKR = p["XKR"]
                MGs = [junk, MG1]
                steps = []
                for sub in range(p["nsub"]):
                    mR = [junkR, mgR[1]][sub]
                    steps.append(lambda sub=sub: A("activation", [steR[sub]], [steR[sub]], out=rstde[0:rows, sub:sub + 1], in_=sse[0:rows, sub:sub + 1],
                                                   func=AF.Sqrt, scale=1.0 / D, bias=EPS))
                    steps.append(lambda sub=sub: V("reciprocal", [steR[sub]], [steR[sub]], out=rstde[0:rows, sub:sub + 1], in_=rstde[0:rows, sub:sub + 1]))
                    steps.append(lambda sub=sub, mR=mR: V("scalar_tensor_tensor", [mR, steR[sub], XKR[sub]], [XKR[sub]], out=XK[0:rows, sub, :],
                                                          in0=MGs[sub][0:rows], scalar=rstde[0:rows, sub:sub + 1], in1=XK[0:rows, sub, :],
                                                          op0=ALU.mult, op1=ALU.add))
                    if p["samp"]:
                        steps.append(lambda sub=sub: DQ([XKR[sub]], [ysR], ys, XK[0:rows, sub, :]))
                    else:
                        r0 = tile * NT + sub * 128
                        steps.append(lambda sub=sub, r0=r0: DQ([XKR[sub]], [ypR[r0 // 128]], yp[r0:r0 + 128, :], XK[:, sub, :]))
                return steps

            def epilogue_M(tile):
                epilogue_early(tile)
                for st in epilogue_steps(tile):
                    st()

            def seg_uproj(tile):
                p = tp_(tile); samp = p["samp"]; nb = p["nb"]; Utok4 = p["Utok4"]; UtokR = p["UtokR"]
                if samp:
                    V("memset", [], [UtokR], Utok4[0:NS, :, 1:8, :], 0.0)
                    for k in range(KT):
                        PE("matmul", [hTR, WinR], [bankR[1]], banks[1][0:nb, :], lhsT=hT[:, k, 0:NS], rhs=Win[:, k, 0:512],
                           start=(k == 0), stop=(k == KT - 1), signal=(k == KT - 1))
                    V("tensor_copy", [bankR[1]], [UtokR], out=Utok4[0:nb, :, 0, :], in_=banks[1][0:nb, :].rearrange("n (g h) -> n g h", g=32))
                    return
                for c in range(4):
                    bk = 1 + (c % 2)
                    for k in range(KT):
                        PE("matmul", [hTR, WinR], [bankR[bk]], banks[bk][:, 0:NT], lhsT=Win[:, k, c * 128:(c + 1) * 128], rhs=hT[:, k, 0:NT],
                           start=(k == 0), stop=(k == KT - 1), signal=(k == KT - 1))
                    if c % 2 == 0:
                        V("tensor_copy", [bankR[bk]], [aTR], out=aT[:, c, :], in_=banks[bk][:, 0:NT])
                    else:
                        A("copy", [bankR[bk]], [aTR], out=aT[:, c, :], in_=banks[bk][:, 0:NT])
                for c in range(4):
                    bk = 1 + (c % 2)
                    tpc = bfv(banks[bk])
                    for j in range(8):
                        PE("transpose", [aTR, cR], [bankR[bk]], tpc[0:NB, j * 128:(j + 1) * 128], aT[:, c, j:NT:8], identb)
                    src = tpc[0:NB, 0:1024].rearrange("n (j g h) -> n j g h", j=8, g=8, h=16)
                    dst = Utok4[0:NB, 8 * c:8 * c + 8, :, :].rearrange("n g j h -> n j g h")
                    if c % 2 == 0:
                        V("tensor_copy", [bankR[bk]], [UtokR], out=dst, in_=src)
                    else:
                        A("copy", [bankR[bk]], [UtokR], out=dst, in_=src)

            def seg_ublk_x(tile):
                p = tp_(tile); samp = p["samp"]; nb = p["nb"]; Utok4 = p["Utok4"]; UtokR = p["UtokR"]
                Ublk = p["Ublk"]; UblkR = p["UblkR"]; XS = p["XS"]; XSR = p["XSR"]; VB = p["VB"]; vbR = p["vbR"]
                for g in range(32):
                    PE("transpose", [UtokR, cR], [bankR[0]], tpv[:, g * 32:g * 32 + nb], Utok4[0:nb, g].rearrange("n j h -> n (j h)"),
                       identb[0:nb, 0:nb])
                V("tensor_copy", [bankR[0]], [UblkR], out=Ublk[:, :, 0:nb],
                  in_=tpv[:, 0:1024].rearrange("p (g n) -> p g n", g=32)[:, :, 0:nb])
                if samp:
                    for ri in range(2):
                        DQ([], [H0tokR], H0tok[0:NS, :], (sre, sim)[ri][l])
                        for gp in range(16):
                            PE("transpose", [H0tokR, cR], [bankR[3]], banks[3][:, gp * NS:(gp + 1) * NS],
                               H0tok[0:NS, gp * 128:(gp + 1) * 128], identf[0:NS, 0:NS])
                        V("tensor_copy", [bankR[3]], [H0R], out=H0[:, :, ri, :].rearrange("p t g -> p g t"),
                          in_=banks[3][:, 0:16 * NS].rearrange("p (g t) -> p g t", g=16))
                xdst = XSs if samp else XS
                xdR = XSsR if samp else XSR
                for q in range(2):
                    bk = (3, 1)[q % 2]
                    for i in range(8):
                        gp = 8 * q + i
                        for a in range(2):
                            g = 2 * gp + a
                            for ri, PT_ in enumerate((PTR, PTI)):
                                PE("matmul", [UblkR, tabR], [bankR[bk]],
                                   banks[bk][a * 64:(a + 1) * 64, (i * 2 + ri) * 32:(i * 2 + ri) * 32 + nb],
                                   lhsT=PT_[:, gp, a * 64:(a + 1) * 64], rhs=Ublk[:, g, 0:nb], start=True, stop=True,
                                   signal=(i == 7 and a == 1 and ri == 1))
                    V("tensor_copy", [bankR[bk]], [xdR], out=xdst[:, 0:nb, :, 8 * q:8 * q + 8].rearrange("p n r g -> p g r n"),
                      in_=banks[bk][:, :].rearrange("p (g r n) -> p g r n", g=8, r=2)[:, :, :, 0:nb])
                if not samp:
                    XSe = XS[:, 0:NB:2]; XSo = XS[:, 1:NB:2]
                    cb = lambda C_, w: bcast_ap(C_[:, w].rearrange("p r g -> p (r g)"), 1, NB // 2)
                    G("tensor_tensor", [XSR, tabR], [vbR], out=fl(VB), in0=fl(XSo), in1=cb(CINV, 0), op=ALU.mult, nosync=True)
                    G("tensor_tensor", [XSR, tabR], [XSR], out=fl(XSo), in0=fl(XSo), in1=cb(CINV, 1), op=ALU.mult, nosync=True)
                    G("tensor_tensor", [vbR, XSR], [vbR], out=VB, in0=VB, in1=swap2(XSo, 2), op=ALU.add, nosync=True)
                    G("tensor_tensor", [vbR, XSR], [vbR], out=VB, in0=VB, in1=XSe, op=ALU.add, nosync=True)
                    for m in range(nb // 2):
                        prev = SC if m == 0 else XS[:, 2 * m - 1]
                        G("tensor_tensor", [scR, XSR, scanR, vbR], [scanR], out=TT, in0=prev, in1=VB[:, m], op=ALU.add, nosync=True)
                        G("tensor_tensor", [scanR, tabR], [scanR], out=P12, in0=COEF16, in1=bcast_ap(TT, 1, 2), op=ALU.mult, nosync=True)
                        G("tensor_tensor", [scanR], [XSR], out=XS[:, 2 * m + 1], in0=P12[:, 0], in1=swap2(P12[:, 1]), op=ALU.add, nosync=True)
                    G("tensor_tensor", [scR, XSR, vbR], [vbR], out=VB[:, 0], in0=SC, in1=XS[:, 0], op=ALU.add, nosync=True)
                    G("tensor_tensor", [XSR, vbR], [vbR], out=VB[:, 1:NB // 2], in0=XS[:, 1:NB - 1:2], in1=XS[:, 2:NB:2], op=ALU.add, nosync=True)
                    G("tensor_tensor", [vbR, tabR, XSR], [XSR], out=fl(XSe), in0=fl(VB), in1=cb(COEF8, 0), op=ALU.mult, nosync=True)
                    G("tensor_tensor", [vbR, tabR], [vbR], out=fl(VB), in0=fl(VB), in1=cb(COEF8, 1), op=ALU.mult, nosync=True)
                    G("tensor_tensor", [XSR, vbR], [XSR], out=XSe, in0=XSe, in1=swap2(VB, 2), op=ALU.add, nosync=True)
                    G("tensor_copy", [scR], [HbR], out=Hb[:, :, :, 0], in_=SC, nosync=True)
                    G("tensor_copy", [XSR], [HbR], out=Hb[:, 0, :, 1:nb].rearrange("p g n -> p n g"), in_=XS[:, 0:nb - 1, 0, :], nosync=True)
                    G("tensor_copy", [XSR], [HbR], out=Hb[:, 1, :, 1:nb].rearrange("p g n -> p n g"), in_=XS[:, 0:nb - 1, 1, :], nosync=True)
                    G("tensor_copy", [XSR], [scR], out=SC, in_=XS[:, nb - 1], nosync=True)
                    if p["last"]:
                        DQ([scR], [Res()], pre[l].rearrange("(gp a) p -> (a p) gp", a=2), SC[:, 0, :], allow_slow_non_contiguous=True)
                        DQ([scR], [Res()], pim[l].rearrange("(gp a) p -> (a p) gp", a=2), SC[:, 1, :], allow_slow_non_contiguous=True)

            def seg_vg(tile):
                p = tp_(tile); samp = p["samp"]; nt = p["nt"]
                for c in range(4):
                    for k in range(KT):
                        PE("matmul", [hTR, WinR], [bankR[1]], banks[1][:, 0:nt], lhsT=Win[:, k, 1024 + c * 128:1024 + (c + 1) * 128],
                           rhs=hT[:, k, 0:nt], start=(k == 0), stop=(k == KT - 1), signal=(k == KT - 1))
                    A("activation", [bankR[1]], [SGR], out=SG[:, 0:nt], in_=banks[1][:, 0:nt], func=AF.Sigmoid)
                    for k in range(KT):
                        PE("matmul", [hTR, WinR], [bankR[2]], banks[2][:, 0:nt], lhsT=Win[:, k, 512 + c * 128:512 + (c + 1) * 128],
                           rhs=hT[:, k, 0:nt], start=(k == 0), stop=(k == KT - 1), signal=(k == KT - 1))
                    if samp:
                        V("tensor_tensor", [bankR[2], SGR], [ZFtR], out=ZFt[:, c, 0:NS], in0=banks[2][:, 0:NS], in1=SG[:, 0:NS], op=ALU.mult)
                    else:
                        V("tensor_tensor", [bankR[2], SGR], [zbR], out=zb[:, c, 30:30 + NT], in0=banks[2][:, 0:NT], in1=SG[:, 0:NT], op=ALU.mult)
                        if p["last"]:
                            V("tensor_tensor", [bankR[2], SGR], [ZFtR], out=ZFt[:, c, 0:30], in0=banks[2][:, NT - 30:NT], in1=SG[:, NT - 30:NT], op=ALU.mult)

            def seg_conv(tile):
                p = tp_(tile); samp = p["samp"]
                if samp:
                    for rt in range(4):
                        DQ([], [bufSR], bufS, scv[l, 4 * rt:4 * rt + 4].rearrange("t k c -> (t k) c"))
                        for c in range(4):
                            PE("transpose", [bufSR, cR], [bankR[2]], banks[2][:, c * 120:(c + 1) * 120], bufS[:, c * 128:(c + 1) * 128],
                               identf[0:120, 0:120])
                        V("tensor_copy", [bankR[2]], [bufTR], out=bufT[:, :, 4 * rt:4 * rt + 4, :].rearrange("p c t k -> p c (t k)"),
                          in_=banks[2][:, 0:480].rearrange("p (c x) -> p c x", c=4))
                    for c in range(4):
                        V("tensor_tensor", [bufTR, pR], [bufTR], out=bufT[:, c], in0=bufT[:, c], in1=bcast_ap(P512[:, l, c, 0:30], 1, NS), op=ALU.mult)
                        V("tensor_reduce", [bufTR], [redR], out=RED, in_=bufT[:, c], axis=AX.X, op=ALU.add)
                        V("scalar_tensor_tensor", [ZFtR, pR, redR], [CCR], out=CC[:, c, 0:NS], in0=ZFt[:, c, 0:NS], scalar=P512[:, l, c, 30:31],
                          in1=RED, op0=ALU.mult, op1=ALU.add)
                        V("tensor_scalar", [CCR, pR], [CCR], out=CC[:, c, 0:NS], in0=CC[:, c, 0:NS], scalar1=P512[:, l, c, 31:32], scalar2=None, op0=ALU.add)
                else:
                    for c in range(4):
                        bk = 1 + (c % 2)
                        for k in range(31):
                            PE("matmul", [zbR, DgR], [bankR[bk]], banks[bk][:, 0:NT], lhsT=Dg[:, c, k, :], rhs=zb[:, c, k:k + NT],
                               start=(k == 0), stop=(k == 30), signal=(k == 30))
                        A("activation", [bankR[bk], pR], [CCR], out=CC[:, c, :], in_=banks[bk][:, 0:NT], func=AF.Identity, bias=P512[:, l, c, 31:32])
                    for c in range(4):
                        V("tensor_copy", [zbR], [zbR], out=zb[:, c, 0:30], in_=zb[:, c, NT:NT + 30])

            def seg_ln(tile):
                p = tp_(tile); nt = p["nt"]; mixT = p["mixT"]; mixR = p["mixR"]
                for c in range(4):
                    PE("matmul", [CCR, cR], [bankR[3]], banks[3][:, 0:nt], lhsT=onesf, rhs=CC[:, c, 0:nt], start=(c == 0), stop=(c == 3), signal=(c == 3))
                for c in range(4):
                    A("activation", [CCR], [CSQR], out=CSQ[:, 0:nt], in_=CC[:, c, 0:nt], func=AF.Square)
                    PE("matmul", [CSQR, cR], [bankR[2]], banks[2][:, 0:nt], lhsT=onesf, rhs=CSQ[:, 0:nt], start=(c == 0), stop=(c == 3), signal=True)
                V("tensor_scalar", [bankR[3]], [lnR], out=MEAN[:, 0:nt], in0=banks[3][:, 0:nt], scalar1=1.0 / 512, scalar2=None, op0=ALU.mult)
                V("tensor_tensor", [lnR], [ltR], out=LT[:, 0:nt], in0=MEAN[:, 0:nt], in1=MEAN[:, 0:nt], op=ALU.mult)
                V("scalar_tensor_tensor", [bankR[2], ltR], [lnR], out=RS[:, 0:nt], in0=banks[2][:, 0:nt], scalar=1.0 / 512, in1=LT[:, 0:nt],
                  op0=ALU.mult, op1=ALU.subtract)
                A("activation", [lnR], [lnR], out=RS[:, 0:nt], in_=RS[:, 0:nt], func=AF.Sqrt, bias=EPS)
                V("reciprocal", [lnR], [lnR], out=RS[:, 0:nt], in_=RS[:, 0:nt])
                for c in range(4):
                    V("tensor_tensor", [CCR, lnR], [ltR], out=LT[:, 0:nt], in0=CC[:, c, 0:nt], in1=MEAN[:, 0:nt], op=ALU.subtract)
                    V("tensor_tensor", [ltR, lnR], [ltR], out=LT[:, 0:nt], in0=LT[:, 0:nt], in1=RS[:, 0:nt], op=ALU.mult)
                    A("activation", [ltR, pR], [mixR], out=mixT[:, 4 + c, 0:nt], in_=LT[:, 0:nt], func=AF.Silu,
                      scale=P512[:, l, c, 32:33], bias=P512[:, l, c, 33:34])

            def seg_scan_tail(tile):
                p = tp_(tile); samp = p["samp"]; nb = p["nb"]; XS = p["XS"]; XSR = p["XSR"]; VB = p["VB"]; vbR = p["vbR"]
                if samp:
                    V("tensor_copy", [H0R], [HbR], out=Hb[:, :, :, 0:NS].rearrange("p r g t -> p t r g"), in_=H0)
                    V("tensor_tensor", [H0R, XSsR], [scanR, XSbR[1]], out=TTs, in0=H0, in1=XSs, op=ALU.add)
                    V("tensor_tensor", [scanR, tabR, XSbR[1]], [scanR, junkR], out=P1s.rearrange("p t r g -> p t (r g)"),
                      in0=TTs.rearrange("p t r g -> p t (r g)"), in1=bcast_ap(A1_1.rearrange("p r g -> p (r g)"), 1, NS), op=ALU.mult)
                    V("tensor_tensor", [scanR, tabR, XSbR[1]], [scanR, junkR], out=P2s.rearrange("p t r g -> p t (r g)"),
                      in0=TTs.rearrange("p t r g -> p t (r g)"), in1=bcast_ap(A2_1.rearrange("p r g -> p (r g)"), 1, NS), op=ALU.mult)
                    V("tensor_tensor", [scanR, junkR], [XSsR], out=XSs[:, :, 0, :], in0=P1s[:, :, 0, :], in1=P2s[:, :, 1, :], op=ALU.add)
                    V("tensor_tensor", [scanR, junkR], [XSsR], out=XSs[:, :, 1, :], in0=P1s[:, :, 1, :], in1=P2s[:, :, 0, :], op=ALU.add)
                else:
                    pass

            def seg_sample_state_out():
                for ri, dst in enumerate((sre_o, sim_o)):
                    for q in range(4):
                        bk = (0, 3)[q % 2]
                        for i in range(4):
                            gp = 4 * q + i
                            PE("transpose", [XSsR, cR], [bankR[bk]], banks[bk][0:NS, i * 128:(i + 1) * 128], XSs[:, :, ri, gp], identf)
                        V("tensor_copy", [bankR[bk], HbR, H0R], [STOR], out=STO[0:NS, q * 512:(q + 1) * 512], in_=banks[bk][0:NS, :])
                    DQ([STOR], [STOR], dst[l], STO[0:NS, :])

            def seg_ytok(tile):
                p = tp_(tile); nb = p["nb"]; Ublk = p["Ublk"]; UblkR = p["UblkR"]; atok5 = p["atok5"]; atokR = p["UtokR"]
                for q in range(8):
                    bk = (3, 2, 1)[q % 3]
                    for i in range(4):
                        g = 4 * q + i; gp = g // 2; a = g % 2
                        o = banks[bk][0:nb, i * 128:(i + 1) * 128]
                        PE("matmul", [UblkR, tabR], [bankR[bk]], o, lhsT=Ublk[:, g, 0:nb], rhs=Tb[:, g, :], start=True, stop=False, signal=False)
                        PE("matmul", [HbR, tabR], [bankR[bk]], o, lhsT=Hb[:, 0, gp, 0:nb], rhs=W3R[a][:, gp, :], start=False, stop=False, signal=False)
                        PE("matmul", [HbR, tabR], [bankR[bk]], o, lhsT=Hb[:, 1, gp, 0:nb], rhs=W3I[a][:, gp, :], start=False, stop=True,
                           signal=(i == 3))
                    gl0 = (q % 2) * 4
                    A("activation", [bankR[bk], UblkR], [atokR], out=atok5[0:nb, q // 2, :, gl0:gl0 + 4, :].rearrange("n j g h -> n g j h"),
                      in_=banks[bk][0:nb, :].rearrange("n (g j h) -> n g j h", g=4, j=8, h=16), func=AF.Gelu)

            def seg_aT_glu(tile):
                p = tp_(tile); samp = p["samp"]; nb = p["nb"]; nt = p["nt"]; atok5 = p["atok5"]; atokR = p["UtokR"]
                mixT = p["mixT"]; mixR = p["mixR"]
                nj = 1 if samp else 8
                for c in range(4):
                    for j in range(nj):
                        PE("transpose", [atokR, cR], [bankR[0]], tpv[:, (c * 8 + j) * 32:(c * 8 + j) * 32 + nb],
                           atok5[0:nb, c, j].rearrange("n g h -> n (g h)"), identb[0:nb, 0:nb])
                if samp:
                    V("tensor_copy", [bankR[0]], [aTR], out=aT[:, :, 0:NS], in_=tpv[:, 0:1024].rearrange("p (c x) -> p c x", c=4)[:, :, 0:NS])
                else:
                    V("tensor_copy", [bankR[0]], [aTR], out=aT.rearrange("p c (n j) -> p c j n", j=8),
                      in_=tpv[:, 0:1024].rearrange("p (c j n) -> p c j n", c=4, j=8))
                for mc in range(4):
                    bk = 1 + (mc % 2)
                    for kc in range(4):
                        PE("matmul", [aTR, WgluR], [bankR[bk]], banks[bk][:, 0:nt], lhsT=Wglu[:, kc, mc * 128:(mc + 1) * 128], rhs=aT[:, kc, 0:nt],
                           start=(kc == 0), stop=(kc == 3), signal=(kc == 3))
                    A("activation", [bankR[bk], pR], [SGR], out=SG[:, 0:nt], in_=banks[bk][:, 0:nt], func=AF.Sigmoid, bias=P512[:, l, mc, 34:35])
                    V("tensor_tensor", [aTR, SGR], [mixR], out=mixT[:, mc, 0:nt], in0=aT[:, mc, 0:nt], in1=SG[:, 0:nt], op=ALU.mult)

            def seg_wout(tile):
                p = tp_(tile); rows = p["rows"]; mixT = p["mixT"]; mixR = p["mixR"]
                for sub in range(p["nsub"]):
                    pp, ppR = pairsM[sub % 2]
                    for hh in range(2):
                        for k in range(KT):
                            PE("matmul", [mixR, WoutR], ppR, pp[0:rows, hh * 512:(hh + 1) * 512], lhsT=mixT[:, k, sub * 128:sub * 128 + rows],
                               rhs=Wout[:, k, hh * 512:(hh + 1) * 512], start=(k == 0), stop=(k == KT - 1), signal=(hh == 1 and k == KT - 1))

            def seg_state_out(tile):
                p = tp_(tile)
                if p["samp"]:
                    for c in range(4):
                        PE("transpose", [ZFtR, cR], [bankR[0]], banks[0][0:NS, c * 128:(c + 1) * 128], ZFt[:, c, 0:NS], identf)
                    V("tensor_copy", [bankR[0]], [ztokR], out=ZTOK[0:NS, :], in_=banks[0][0:NS, :])
                    DQ([ztokR], [ztokR], scv_o[l, :, 29, :], ZTOK[0:NS, :])
                    DQ([], [Res()], scv_o[l, :, 0:29, :], scv[l, :, 1:30, :])
                elif p["last"]:
                    for c in range(4):
                        PE("transpose", [ZFtR, cR], [bankR[0]], banks[0][0:30, c * 128:(c + 1) * 128], ZFt[:, c, 0:30], identf)
                    V("tensor_copy", [bankR[0]], [ztokR], out=ZTOK[:, :], in_=banks[0][0:30, :])
                    DQ([ztokR], [ztokR], pcv[l], ZTOK[:, :])

            pending = []

            def drip(k):
                for _ in range(k):
                    if pending:
                        pending.pop(0)()

            front_a(0)
            front_b(0)
            xk_load(0)
            seg_uproj(0); front_a(1); seg_ublk_x(0); seg_vg(0); seg_conv(0); seg_ln(0)
            seg_scan_tail(0)
            for t in range(NTILES):
                if t + 1 < NTILES:
                    seg_ytok(t); drip(3)
                    front_b(t + 1); drip(3)
                    pending.extend(front_a_steps(t + 2))
                    seg_aT_glu(t); drip(2)
                    seg_uproj(t + 1); drip(4)
                    seg_ublk_x(t + 1); drip(4)
                    seg_vg(t + 1); drip(4)
                    seg_conv(t + 1); drip(4)
                    seg_ln(t + 1); drip(100)
                    seg_scan_tail(t + 1)
                    seg_state_out(t + 1)
                    xk_load(t)
                    seg_wout(t)
                    epilogue_early(t)
                    pending.extend(epilogue_steps(t))
                else:
                    seg_ytok(t); drip(100); seg_aT_glu(t); xk_load(t); seg_wout(t); epilogue_M(t)
            s_ = NTILES
            front_b(s_); seg_uproj(s_); seg_ublk_x(s_); seg_vg(s_); seg_conv(s_); seg_ln(s_); seg_scan_tail(s_)
            seg_ytok(s_); seg_aT_glu(s_); xk_load(s_); seg_wout(s_); epilogue_M(s_); seg_state_out(s_); seg_sample_state_out()
            S.barrier()
            print("phase M arena peak", AR.peak)
            AR.reset(mM)

            Wup = AR.alloc(BF16, [128, KT, 2 * DFF]); WupQ = [Res("Wup%d" % i) for i in range(8)]
            Wdn = AR.alloc(BF16, [128, 22, D]); WdnQ = [Res("Wdn%d" % i) for i in range(22)]
            dn_next = 0
            for qi in range(4):
                for half in range(2):
                    q = qi + 4 * half
                    for k in range(KT):
                        DW([], [WupQ[q]], Wup[:, k, 704 * q:704 * (q + 1)], w_up[l][k * 128:(k + 1) * 128, 704 * q:704 * (q + 1)])
                nd = 6 if qi < 3 else 4
                for c in range(dn_next, dn_next + nd):
                    DW([], [WdnQ[c]], Wdn[:, c, :], w_down[l][c * 128:(c + 1) * 128, :])
                dn_next += nd
            WupAll = WupQ
            gpost = AR.alloc(F32, [128, D]); gpostR = Res("gpost2")
            DQ([], [gpostR], gpost, bass.AP(g_post_ffn.tensor, g_post_ffn[l].offset, [[0, 128], [1, D]]))
            hT = AR.alloc(BF16, [128, KT, 258]); hTR = Res("hT2")
            V("memset", [], [hTR], hT[:, :, 0:2], 0.0)
            XKs = [AR.alloc(F32, [128, 2, D]) for _ in range(2)]
            XKRs = [[Res("XKF%d_%d" % (b, i)) for i in range(2)] for b in range(2)]
            hb2 = AR.alloc(BF16, [128, 2, D]); hbRs = [Res("hbF0"), Res("hbF1")]
            ssf = AR.alloc(F32, [128, 2]); rstdf = AR.alloc(F32, [128, 2]); stfR = [Res("statfF0"), Res("statfF1")]
            junk = AR.alloc(F32, [128, D]); junkR = Res("junk2")
            ss = AR.alloc(F32, [128, 1]); rstd = AR.alloc(F32, [128, 1]); stR = Res("stat2")
            aT2 = AR.alloc(BF16, [128, 4, 256]); aT2R = [Res("aT2_%d" % i) for i in range(4)]
            TG = [AR.alloc(F32, [128, 256]) for _ in range(2)]; TV = [AR.alloc(F32, [128, 256]) for _ in range(2)]
            tgR = [Res("TG0"), Res("TG1")]; tvR = [Res("TV0"), Res("TV1")]
            SGT = [AR.alloc(F32, [128, 256]) for _ in range(2)]; sgtR = [Res("SGT0"), Res("SGT1")]
            RAWv = [AR.alloc(F32, [128, 258]) for _ in range(2)]; rawvR = [Res("RAWv0"), Res("RAWv1")]
            TMPV = [AR.alloc(F32, [128, 256]) for _ in range(2)]; tmpvR = [Res("TMPV0"), Res("TMPV1")]
            RAWg = RAWv; rawgR = rawvR; TMPG = TMPV; tmpgR = tmpvR
            TMPX = junk; tmpxR = junkR
            FBs = AR.alloc(F32, [32, 512]); FBsR = Res("FBs")
            FB = AR.alloc(F32, [128, NUP, NS, 2]); FBR = Res("FB"); fb_off = AR.last_off
            MG1 = AR.alias(fb_off, F32, [128, D])
            sse = AR.alloc(F32, [128, 2]); rstde = AR.alloc(F32, [128, 2]); steR = [Res("steF0"), Res("steF1")]
            UPT = AR.alloc(F32, [NS + 2, 2, 512]); uptR = [Res("UPT0"), Res("UPT1")]

            def tile_paramsF(tile):
                samp = (tile == 8)
                return samp, (1 if samp else 2), (NS if samp else 128)

            def frontF_a(tile):
                samp, nsub, rows = tile_paramsF(tile)
                XK = XKs[tile % 2]; XKR = XKRs[tile % 2]
                for sub in range(nsub):
                    if samp:
                        sap = ys; rres = ysR
                    else:
                        r0 = tile * 256 + sub * 128
                        sap = yp[r0:r0 + 128, :]; rres = ypR[r0 // 128]
                    DQ([rres], [XKR[sub]], XK[0:rows, sub, :], sap)
                    A("activation", [XKR[sub]], [junkR], out=junk[0:rows], in_=XK[0:rows, sub, :], func=AF.Square)
                    V("tensor_reduce", [junkR], [stfR[sub]], out=ssf[0:rows, sub:sub + 1], in_=junk[0:rows], axis=AX.X, op=ALU.add)
                    A("activation", [stfR[sub]], [stfR[sub]], out=rstdf[0:rows, sub:sub + 1], in_=ssf[0:rows, sub:sub + 1], func=AF.Sqrt,
                      scale=1.0 / D, bias=EPS)
                    V("reciprocal", [stfR[sub]], [stfR[sub]], out=rstdf[0:rows, sub:sub + 1], in_=rstdf[0:rows, sub:sub + 1])
                    V("tensor_scalar", [XKR[sub], stfR[sub]], [hbRs[sub]], out=hb2[0:rows, sub, :], in0=XK[0:rows, sub, :],
                      scalar1=rstdf[0:rows, sub:sub + 1], scalar2=None, op0=ALU.mult)

            def frontF_b(tile):
                samp, nsub, rows = tile_paramsF(tile)
                tpv = bfv(banks[0])
                for sub in range(nsub):
                    for k in range(KT):
                        PE("transpose", [hbRs[sub], cR], [bankR[0]], tpv[:, k * 128:k * 128 + rows], hb2[0:rows, sub, k * 128:(k + 1) * 128],
                           identb[0:rows, 0:rows])
                    V("tensor_tensor", [bankR[0], pR], [hTR], out=hT[:, :, 2 + sub * 128:2 + sub * 128 + rows],
                      in0=tpv[:, :].rearrange("p (k t) -> p k t", k=KT)[:, :, 0:rows],
                      in1=bcast_ap(P1024[:, l, :, 1], 2, rows), op=ALU.mult)

            def epilogue_F_early(tile):
                samp, nsub, rows = tile_paramsF(tile)
                pairs = [(b67, [b67R]), (b45, [bankR[4], bankR[5]])]
                MGs = [junk, MG1]
                for sub in range(nsub):
                    pp, ppR = pairs[sub % 2]
                    mR = [junkR, FBR][sub]
                    A("activation", ppR, [mR], out=MGs[sub][0:rows], in_=pp[0:rows, :], func=AF.Square)
                    V("tensor_reduce", [mR], [steR[sub]], out=sse[0:rows, sub:sub + 1], in_=MGs[sub][0:rows], axis=AX.X, op=ALU.add)
                    V("tensor_tensor", ppR + [gpostR], [mR], out=MGs[sub][0:rows], in0=pp[0:rows, :], in1=gpost[0:rows], op=ALU.mult)

            def epilogue_F_steps(tile):
                samp, nsub, rows = tile_paramsF(tile)
                XK = XKs[tile % 2]; XKR = XKRs[tile % 2]
                MGs = [junk, MG1]
                steps = []
                for sub in range(nsub):
                    mR = [junkR, FBR][sub]
                    steps.append(lambda sub=sub: A("activation", [steR[sub]], [steR[sub]], out=rstde[0:rows, sub:sub + 1], in_=sse[0:rows, sub:sub + 1],
                                                   func=AF.Sqrt, scale=1.0 / D, bias=EPS))
                    steps.append(lambda sub=sub: V("reciprocal", [steR[sub]], [steR[sub]], out=rstde[0:rows, sub:sub + 1], in_=rstde[0:rows, sub:sub + 1]))
                    steps.append(lambda sub=sub, mR=mR: V("scalar_tensor_tensor", [mR, steR[sub], XKR[sub]], [XKR[sub]], out=XK[0:rows, sub, :],
                                                          in0=MGs[sub][0:rows], scalar=rstde[0:rows, sub:sub + 1], in1=XK[0:rows, sub, :],
                                                          op0=ALU.mult, op1=ALU.add))
                    if samp:
                        steps.append(lambda sub=sub: DQ([XKR[sub]], [ysR], ys, XK[0:rows, sub, :]))
                    else:
                        r0 = tile * 256 + sub * 128
                        steps.append(lambda sub=sub, r0=r0: DQ([XKR[sub]], [ypR[r0 // 128]], yp[r0:r0 + 128, :], XK[:, sub, :]))
                return steps

            def frontF_a_steps(tile):
                samp, nsub, rows = tile_paramsF(tile)
                XK = XKs[tile % 2]; XKR = XKRs[tile % 2]
                steps = []
                for sub in range(nsub):
                    if samp:
                        sap = ys; rres = ysR
                    else:
                        r0 = tile * 256 + sub * 128
                        sap = yp[r0:r0 + 128, :]; rres = ypR[r0 // 128]
                    steps.append(lambda sub=sub, sap=sap, rres=rres: DQ([rres], [XKR[sub]], XK[0:rows, sub, :], sap))
                for sub in range(nsub):
                    steps.append(lambda sub=sub: A("activation", [XKR[sub]], [junkR], out=junk[0:rows], in_=XK[0:rows, sub, :], func=AF.Square))
                    steps.append(lambda sub=sub: V("tensor_reduce", [junkR], [stfR[sub]], out=ssf[0:rows, sub:sub + 1], in_=junk[0:rows], axis=AX.X, op=ALU.add))
                    steps.append(lambda sub=sub: A("activation", [stfR[sub]], [stfR[sub]], out=rstdf[0:rows, sub:sub + 1], in_=ssf[0:rows, sub:sub + 1],
                                                   func=AF.Sqrt, scale=1.0 / D, bias=EPS))
                    steps.append(lambda sub=sub: V("reciprocal", [stfR[sub]], [stfR[sub]], out=rstdf[0:rows, sub:sub + 1], in_=rstdf[0:rows, sub:sub + 1]))
                    steps.append(lambda sub=sub: V("tensor_scalar", [XKR[sub], stfR[sub]], [hbRs[sub]], out=hb2[0:rows, sub, :], in0=XK[0:rows, sub, :],
                                                   scalar1=rstdf[0:rows, sub:sub + 1], scalar2=None, op0=ALU.mult))
                return steps

            pendF = []

            def dripF(k):
                for _ in range(k):
                    if pendF:
                        pendF.pop(0)()

            frontF_a(0)
            frontF_b(0)
            for tile in range(9):
                samp = (tile == 8)
                nt = NS if samp else 256
                nsub = 1 if samp else 2
                rows = NS if samp else 128
                XK = XKs[tile % 2]; XKR = XKRs[tile % 2]
                pairs = [(b67, [b67R]), (b45, [bankR[4], bankR[5]])]

                def down(c, nsub=nsub, rows=rows, pairs=pairs):
                    for sub in range(nsub):
                        pp, ppR = pairs[sub % 2]
                        for hh in range(2):
                            PE("matmul", [aT2R[c % 4], WdnQ[c]], ppR, pp[0:rows, hh * 512:(hh + 1) * 512], lhsT=aT2[:, c % 4, sub * 128:sub * 128 + rows],
                               rhs=Wdn[:, c, hh * 512:(hh + 1) * 512], start=(c == 0), stop=(c == 21), signal=True, skip_group_check=True)
                if samp:
                    dripF(100)

                    def fb_round(q):
                        DQ([], [FBsR], FBs[:, 0:512], sff[l].rearrange("t r c -> (t r) c")[:, q * 512:(q + 1) * 512])
                        for i in range(4):
                            PE("transpose", [FBsR, cR], [bankR[0]], banks[0][:, i * 32:(i + 1) * 32], FBs[:, i * 128:(i + 1) * 128], identf[0:32, 0:32])
                        V("tensor_copy", [bankR[0]], [FBR], out=FB[:, 4 * q:4 * q + 4].rearrange("p c t r -> p c (t r)"),
                          in_=banks[0][:, 0:128].rearrange("p (c x) -> p c x", c=4))
                        fq = FB[:, 4 * q:4 * q + 4]
                        pq = P5632[:, l, 4 * q:4 * q + 4, :]
                        V("tensor_tensor", [FBR, pR], [FBR], out=fq[:, :, :, 0], in0=fq[:, :, :, 0], in1=bcast_ap(pq[:, :, 0], 2, NS), op=ALU.mult)
                        V("tensor_tensor", [FBR, pR], [FBR], out=fq[:, :, :, 1], in0=fq[:, :, :, 1], in1=bcast_ap(pq[:, :, 1], 2, NS), op=ALU.mult)
                        V("tensor_tensor", [FBR], [FBR], out=fq[:, :, :, 0], in0=fq[:, :, :, 0], in1=fq[:, :, :, 1], op=ALU.add)
                        V("tensor_tensor", [FBR, pR], [FBR], out=fq[:, :, :, 0], in0=fq[:, :, :, 0], in1=bcast_ap(pq[:, :, 3], 2, NS), op=ALU.add)

                    def upt_round(q):
                        bk = 1 + (q % 2)
                        for k in range(KT):
                            PE("matmul", [hTR] + WupAll, [bankR[bk]], banks[bk][0:2 + NS, :], lhsT=hT[:, k, 0:2 + NS], rhs=Wup[:, k, q * 512:(q + 1) * 512],
                               start=(k == 0), stop=(k == KT - 1), signal=(k == KT - 1))
                        V("tensor_copy", [bankR[bk]], [uptR[q % 2]], out=UPT[0:2 + NS, q % 2, :], in_=banks[bk][0:2 + NS, :])
                        DW([uptR[q % 2]], [uptR[q % 2]], sff_o[l, :, 1, q * 512:(q + 1) * 512], UPT[2:2 + NS, q % 2, :])
                        DW([uptR[q % 2]], [uptR[q % 2]], pff[l, :, q * 512:(q + 1) * 512], UPT[0:2, q % 2, :])

                    DQ([], [Res()], sff_o[l, :, 0, :], sff[l, :, 1, :])
                    fb_round(0)
                    for q in range(11):
                        if q + 1 < 11:
                            fb_round(q + 1)
                        upt_round(q)
                for c in range(22):
                    par = c % 2
                    if c == 4 and tile < 8:
                        pendF.extend(frontF_a_steps(tile + 1))
                    for (ch, bk, Tt, TtR, isval) in ((c, (1, 3)[par], TG[par], tgR[par], False), (22 + c, (2, 0)[par], TV[par], tvR[par], True)):
                        RW = RAWv if isval else RAWg; rwR = rawvR if isval else rawgR
                        TMP = TMPV if isval else TMPG; tmpR = tmpvR if isval else tmpgR
                        w0 = P5632[:, l, ch, 0:1]; w1 = P5632[:, l, ch, 1:2]; w2 = P5632[:, l, ch, 2:3]; bb = P5632[:, l, ch, 3:4]
                        ps = banks[bk]
                        if samp:
                            for k in range(KT):
                                PE("matmul", [hTR, WupQ[(ch * 128) // 704], WupQ[(ch * 128 + 127) // 704]], [bankR[bk]], ps[:, 0:NS], lhsT=Wup[:, k, ch * 128:(ch + 1) * 128], rhs=hT[:, k, 2:2 + NS],
                                   start=(k == 0), stop=(k == KT - 1), signal=(k == KT - 1))
                            V("scalar_tensor_tensor", [bankR[bk], FBR, pR], [TtR], out=Tt[:, 0:NS], in0=ps[:, 0:NS], scalar=w2, in1=FB[:, ch, :, 0],
                              op0=ALU.mult, op1=ALU.add)
                        else:
                            for k in range(KT):
                                PE("matmul", [hTR, WupQ[(ch * 128) // 704], WupQ[(ch * 128 + 127) // 704]], [bankR[bk]], ps[:, 0:nt + 2], lhsT=Wup[:, k, ch * 128:(ch + 1) * 128], rhs=hT[:, k, 0:nt + 2],
                                   start=(k == 0), stop=(k == KT - 1), signal=(k == KT - 1))
                            A("activation", [bankR[bk], pR], [TtR], out=Tt[:, 0:nt], in_=ps[:, 2:nt + 2], func=AF.Identity, scale=w2, bias=bb)
                            V("scalar_tensor_tensor", [bankR[bk], pR, TtR], [TtR], out=Tt[:, 0:nt], in0=ps[:, 0:nt], scalar=w0, in1=Tt[:, 0:nt],
                              op0=ALU.mult, op1=ALU.add)
                            if isval:
                                A("copy", [bankR[bk]], [rwR[par]], out=RW[par][:, 0:nt + 2], in_=ps[:, 0:nt + 2])
                                G("tensor_scalar", [rwR[par], pR], [tmpR[par]], out=TMP[par][:, 0:nt], in0=RW[par][:, 1:nt + 1], scalar1=w1, scalar2=0.0,
                                  op0=ALU.mult, op1=ALU.add)
                                G("tensor_tensor", [tmpR[par], TtR], [TtR], out=Tt[:, 0:nt], in0=Tt[:, 0:nt], in1=TMP[par][:, 0:nt], op=ALU.add)
                            else:
                                V("scalar_tensor_tensor", [bankR[bk], pR, TtR], [TtR], out=Tt[:, 0:nt], in0=ps[:, 1:nt + 1], scalar=w1, in1=Tt[:, 0:nt],
                                  op0=ALU.mult, op1=ALU.add)
                    A("activation", [tgR[par]], [sgtR[par]], out=SGT[par][:, 0:nt], in_=TG[par][:, 0:nt], func=AF.Silu)
                    V("tensor_tensor", [sgtR[par], tvR[par]], [aT2R[c % 4]], out=aT2[:, c % 4, 0:nt], in0=SGT[par][:, 0:nt], in1=TV[par][:, 0:nt], op=ALU.mult)
                    if c < 19:
                        dripF(2 if c < 4 else 1)
                    elif c == 19:
                        dripF(100)
                    if c >= 2:
                        down(c - 2)
                    if c == 21:
                        if not samp:
                            V("tensor_copy", [hTR], [hTR], out=hT[:, :, 0:2], in_=hT[:, :, nt:nt + 2])
                        if tile < 8:
                            frontF_b(tile + 1)
                        down(20)
                        down(21)
                epilogue_F_early(tile)
                pendF.extend(epilogue_F_steps(tile))
            dripF(100)
            S.barrier()
            AR.reset(mM)

        S.barrier()
        with nc.Block() as block:
            S.emit(block)
    return nc


_NC_CACHE = {}

_WNAMES = ["g_pre_mix", "w_in", "lam_re", "lam_im", "log_dt", "b_re", "b_im", "c_re", "c_im", "d_skip", "w_glu",
           "b_glu", "conv_w", "conv_b", "ln_g", "ln_b", "w_out", "g_post_mix", "g_pre_ffn", "w_up", "ffn_conv_w",
           "ffn_conv_b", "w_down", "g_post_ffn"]


def kernel(**inputs):
    n = 8
    f = lambda a: np.ascontiguousarray(np.asarray(a, dtype=np.float32))
    W = {k: f(inputs[k]) for k in _WNAMES}
    W["c_re"] = W["c_re"].reshape(DEPTH, 512, 64)
    W["c_im"] = W["c_im"].reshape(DEPTH, 512, 64)
    ident = np.eye(128, dtype=np.float32)
    jj = np.arange(128) // 16
    mask = (jj[None, :] >= jj[:, None]).astype(np.float32)
    x_prompt = f(inputs["x_prompt"]); x_sample = f(inputs["x_sample"])
    s_re = f(inputs["state_ssm_re"]); s_im = f(inputs["state_ssm_im"])
    s_cv = f(inputs["state_conv"]); s_ff = f(inputs["state_ffn_conv"])
    in_maps = []
    for c in range(n):
        sl = slice(c * NS, (c + 1) * NS)
        m = dict(W)
        m["xp"] = x_prompt[c]
        m["xs"] = np.ascontiguousarray(x_sample[sl, 0, :])
        m["sre"] = np.ascontiguousarray(s_re[:, sl].reshape(DEPTH, NS, 2048))
        m["sim"] = np.ascontiguousarray(s_im[:, sl].reshape(DEPTH, NS, 2048))
        m["scv"] = np.ascontiguousarray(s_cv[:, sl])
        m["sff"] = np.ascontiguousarray(s_ff[:, sl])
        m["c_ident"] = ident
        m["c_mask"] = mask
        in_maps.append(m)
    if "nc" not in _NC_CACHE:
        _NC_CACHE["nc"] = build_program()
    nc = _NC_CACHE["nc"]
    res = run_bass_kernel_spmd(nc, in_maps, core_ids=list(range(n)))
    R = res.results
    y_prompt = np.stack([R[c]["yp"] for c in range(n)], axis=0)
    y_sample = np.concatenate([R[c]["ys"] for c in range(n)], axis=0)[:, None, :]
    p_re = np.stack([R[c]["pre"] for c in range(n)], axis=1)
    p_im = np.stack([R[c]["pim"] for c in range(n)], axis=1)
    p_cv = np.stack([R[c]["pcv"] for c in range(n)], axis=1)
    p_ff = np.stack([R[c]["pff"] for c in range(n)], axis=1)
    s_re_o = np.concatenate([R[c]["sre_o"].reshape(DEPTH, NS, 32, 64) for c in range(n)], axis=1)
    s_im_o = np.concatenate([R[c]["sim_o"].reshape(DEPTH, NS, 32, 64) for c in range(n)], axis=1)
    s_cv_o = np.concatenate([R[c]["scv_o"] for c in range(n)], axis=1)
    s_ff_o = np.concatenate([R[c]["sff_o"] for c in range(n)], axis=1)
    outs = (y_prompt, y_sample, p_re, p_im, p_cv, p_ff, s_re_o, s_im_o, s_cv_o, s_ff_o)
    return tuple(np.ascontiguousarray(o, dtype=np.float32) for o in outs)
```
